# Optimizing a Trainium2 kernel written in Bass

```python
import jax, jax.numpy as jnp
from jax import lax
import numpy as np


D_MODEL = 2048
BATCH = 8
SEQ = 2048
DEPTH = 2
DEC_BATCH = 32
DEC_SEQ = 16
PAST_LEN = 4096

CHUNK = 64
N_EVEN = (DEPTH + 1) // 2
N_ODD = DEPTH // 2
D_CONV = D_MODEL // 2
D_POOL = D_MODEL // 2
CONV_WIDTH = 31
POOL_WINDOWS = (2, 4, 8, 16)
N_POOL_GROUPS = 4
POOL_GROUP = D_POOL // N_POOL_GROUPS
POOL_MAX = 16
HGRN_DK = 128
HGRN_DV = 128
HGRN_HEADS = D_MODEL // HGRN_DK
HGRN_DIM = HGRN_HEADS * HGRN_DK
D_FF = 5632
N_SUBNORMS = 6
EPS = 1e-6

kernel_name = "hybrid_streaming_conv_pool_hgrn2_step"


def rms_norm(x, g):
    xf = x.astype(jnp.float32)
    y = xf * lax.rsqrt(jnp.mean(xf * xf, axis=-1, keepdims=True) + EPS)
    return (y * g.astype(jnp.float32)).astype(x.dtype)


def layer_norm(x, g, b):
    xf = x.astype(jnp.float32)
    mu = jnp.mean(xf, axis=-1, keepdims=True)
    xc = xf - mu
    y = xc * lax.rsqrt(jnp.mean(xc * xc, axis=-1, keepdims=True) + EPS)
    return (y * g.astype(jnp.float32) + b.astype(jnp.float32)).astype(x.dtype)


def swiglu_ffn(x, w_in, w_out):
    a, b = jnp.split(x @ w_in, 2, axis=-1)
    return (jax.nn.silu(a) * b) @ w_out


def conformer_conv(a, hist, w, bias, ln_g, ln_b):
    seq = jnp.concatenate([hist.astype(a.dtype), a], axis=1)
    y = lax.conv_general_dilated(seq, w[:, None, :].astype(a.dtype), window_strides=(1,), padding='VALID',
                                 dimension_numbers=('NWC', 'WIO', 'NWC'), feature_group_count=D_CONV)
    y = jax.nn.silu(layer_norm(y + bias.astype(a.dtype), ln_g, ln_b))
    return y, seq[:, -(CONV_WIDTH - 1):]


def multiscale_pool(p, hist, start_pos, w, scale):
    B, L, _ = p.shape
    seq = jnp.concatenate([hist.astype(p.dtype), p], axis=1)
    cs = jnp.concatenate([jnp.zeros((B, 1, D_POOL), jnp.float32),
                          jnp.cumsum(seq.astype(jnp.float32), axis=1)], axis=1)
    hi = cs[:, POOL_MAX:]
    pos = start_pos + jnp.arange(L, dtype=jnp.int32)
    means = []
    for gi, win in enumerate(POOL_WINDOWS):
        sl = slice(gi * POOL_GROUP, (gi + 1) * POOL_GROUP)
        wsum = hi[..., sl] - cs[:, POOL_MAX - win:POOL_MAX - win + L, sl]
        cnt = jnp.minimum(win, pos + 1).astype(jnp.float32)[None, :, None]
        means.append(wsum / cnt)
    y = jnp.concatenate(means, axis=-1) - p.astype(jnp.float32)
    y = jnp.einsum('blgc,gcd->blgd', y.reshape(B, L, N_POOL_GROUPS, POOL_GROUP), w.astype(jnp.float32))
    y = y.reshape(B, L, D_POOL) * scale.astype(jnp.float32)
    return y.astype(p.dtype), seq[:, -(POOL_MAX - 1):]


def _intra_step(s, inp):
    q_t, k_t, v_t, lf_t = inp
    s = jnp.exp(lf_t)[..., None] * s + k_t[..., :, None] * v_t[..., None, :]
    return s, jnp.einsum('bnhk,bnhkv->bnhv', q_t, s)


def _chunk_step(S, inp):
    decay, local = inp
    return decay[..., None] * S + local, S


def hgrn2(xn, S0, w_in, w_out, g_norm, lb):
    B, L, _ = xn.shape
    f32 = jnp.float32
    q, fz, v, gz = jnp.split(xn @ w_in, 4, axis=-1)
    lb = lb.astype(f32).reshape(HGRN_HEADS, HGRN_DK)
    f = lb + (1.0 - lb) * jax.nn.sigmoid(fz.astype(f32).reshape(B, L, HGRN_HEADS, HGRN_DK))
    k = 1.0 - f
    logf = jnp.log(f)
    q = jax.nn.silu(q.astype(f32)).reshape(B, L, HGRN_HEADS, HGRN_DK) * (HGRN_DK ** -0.5)
    v = v.astype(f32).reshape(B, L, HGRN_HEADS, HGRN_DV)
    nc = -(-L // CHUNK)
    pad = nc * CHUNK - L

    def to_chunks(t):
        t = jnp.pad(t, ((0, 0), (0, pad), (0, 0), (0, 0)))
        return t.reshape(B, nc, CHUNK, HGRN_HEADS, t.shape[-1])

    qc, kc, vc, lfc = to_chunks(q), to_chunks(k), to_chunks(v), to_chunks(logf)
    s_zero = jnp.zeros((B, nc, HGRN_HEADS, HGRN_DK, HGRN_DV), f32)
    s_local, o_local = lax.scan(_intra_step, s_zero,
                                (jnp.moveaxis(qc, 2, 0), jnp.moveaxis(kc, 2, 0),
                                 jnp.moveaxis(vc, 2, 0), jnp.moveaxis(lfc, 2, 0)))
    o_local = jnp.moveaxis(o_local, 0, 2)
    b = jnp.cumsum(lfc, axis=2)
    decay = jnp.exp(b[:, :, -1])
    S_final, S_start = lax.scan(_chunk_step, S0.astype(f32),
                                (jnp.moveaxis(decay, 1, 0), jnp.moveaxis(s_local, 1, 0)))
    o_inter = jnp.einsum('bnchk,nbhkv->bnchv', qc * jnp.exp(b), S_start)
    o = (o_local + o_inter).reshape(B, nc * CHUNK, HGRN_HEADS, HGRN_DV)[:, :L]
    o = o * lax.rsqrt(jnp.mean(o * o, axis=-1, keepdims=True) + EPS) * g_norm.astype(f32)
    o = o.reshape(B, L, HGRN_DIM) * jax.nn.silu(gz.astype(f32))
    return o.astype(xn.dtype) @ w_out, S_final.astype(S0.dtype)


def trunk(x, conv_hist, pool_hist, hgrn_S, start_pos, ab_w_in, ab_w_out, conv_w, conv_b, conv_ln_g,
          conv_ln_b, pool_w, pool_scale, hgrn_w_in, hgrn_w_out, hgrn_gnorm, hgrn_lb, ffn_w_in, ffn_w_out,
          norm_g):
    lb_all = jnp.cumsum(jax.nn.softmax(hgrn_lb.astype(jnp.float32), axis=0), axis=0)
    lb_all = lb_all - lb_all[0:1]
    conv_new, pool_new, hgrn_new = [], [], []
    for l in range(DEPTH):
        g = norm_g[l]
        h = swiglu_ffn(rms_norm(x, g[0]), ffn_w_in[l, 0], ffn_w_out[l, 0])
        x = x + 0.5 * rms_norm(h, g[1])
        h = rms_norm(x, g[2])
        if l % 2 == 0:
            e = l // 2
            u = h @ ab_w_in[e]
            a_val, a_gate, p = jnp.split(u, [D_CONV, 2 * D_CONV], axis=-1)
            a = a_val * jax.nn.sigmoid(a_gate)
            a_out, c_new = conformer_conv(a, conv_hist[e], conv_w[e], conv_b[e], conv_ln_g[e], conv_ln_b[e])
            p_out, p_new = multiscale_pool(p, pool_hist[e], start_pos, pool_w[e], pool_scale[e])
            m = jnp.concatenate([a_out, p_out], axis=-1) @ ab_w_out[e]
            conv_new.append(c_new)
            pool_new.append(p_new)
        else:
            o = l // 2
            m, s_new = hgrn2(h, hgrn_S[o], hgrn_w_in[o], hgrn_w_out[o], hgrn_gnorm[o], lb_all[l])
            hgrn_new.append(s_new)
        x = x + rms_norm(m, g[3])
        h = swiglu_ffn(rms_norm(x, g[4]), ffn_w_in[l, 1], ffn_w_out[l, 1])
        x = x + 0.5 * rms_norm(h, g[5])
    return x, jnp.stack(conv_new), jnp.stack(pool_new), jnp.stack(hgrn_new)


def setup_inputs(seed: int = 0) -> dict:
    key = jax.random.key(seed)
    ks = jax.random.split(key, 20)

    def nrm(k, shape, s):
        return jax.random.normal(k, shape, jnp.float32) * s

    return {
        'x_prompt': nrm(ks[0], (BATCH, SEQ, D_MODEL), 1.0),
        'x_sample': nrm(ks[1], (DEC_BATCH, DEC_SEQ, D_MODEL), 1.0),
        'cache_conv': nrm(ks[2], (N_EVEN, DEC_BATCH, CONV_WIDTH - 1, D_CONV), 0.5),
        'cache_pool': nrm(ks[3], (N_EVEN, DEC_BATCH, POOL_MAX - 1, D_POOL), 1.0),
        'state_hgrn': nrm(ks[4], (N_ODD, DEC_BATCH, HGRN_HEADS, HGRN_DK, HGRN_DV), 0.5),
        'ab_w_in': nrm(ks[5], (N_EVEN, D_MODEL, 2 * D_CONV + D_POOL), D_MODEL ** -0.5),
        'ab_w_out': nrm(ks[6], (N_EVEN, D_CONV + D_POOL, D_MODEL), (D_CONV + D_POOL) ** -0.5),
        'conv_w': nrm(ks[7], (N_EVEN, CONV_WIDTH, D_CONV), CONV_WIDTH ** -0.5),
        'conv_b': nrm(ks[8], (N_EVEN, D_CONV), 0.02),
        'conv_ln_g': 1.0 + nrm(ks[9], (N_EVEN, D_CONV), 0.02),
        'conv_ln_b': nrm(ks[10], (N_EVEN, D_CONV), 0.02),
        'pool_w': nrm(ks[11], (N_EVEN, N_POOL_GROUPS, POOL_GROUP, POOL_GROUP), POOL_GROUP ** -0.5),
        'pool_scale': 1.0 + nrm(ks[12], (N_EVEN, D_POOL), 0.1),
        'hgrn_w_in': nrm(ks[13], (N_ODD, D_MODEL, 4 * HGRN_DIM), D_MODEL ** -0.5),
        'hgrn_w_out': nrm(ks[14], (N_ODD, HGRN_DIM, D_MODEL), HGRN_DIM ** -0.5),
        'hgrn_gnorm': 1.0 + nrm(ks[15], (N_ODD, HGRN_DV), 0.02),
        'hgrn_lb': nrm(ks[16], (DEPTH, HGRN_DIM), 0.1),
        'ffn_w_in': nrm(ks[17], (DEPTH, 2, D_MODEL, 2 * D_FF), D_MODEL ** -0.5),
        'ffn_w_out': nrm(ks[18], (DEPTH, 2, D_FF, D_MODEL), D_FF ** -0.5),
        'norm_g': 1.0 + nrm(ks[19], (DEPTH, N_SUBNORMS, D_MODEL), 0.02),
    }


def reference(x_prompt, x_sample, cache_conv, cache_pool, state_hgrn, ab_w_in, ab_w_out, conv_w, conv_b,
              conv_ln_g, conv_ln_b, pool_w, pool_scale, hgrn_w_in, hgrn_w_out, hgrn_gnorm, hgrn_lb, ffn_w_in,
              ffn_w_out, norm_g):
    B = x_prompt.shape[0]
    dt = x_prompt.dtype
    zero_conv = jnp.zeros((N_EVEN, B, CONV_WIDTH - 1, D_CONV), dt)
    zero_pool = jnp.zeros((N_EVEN, B, POOL_MAX - 1, D_POOL), dt)
    zero_hgrn = jnp.zeros((N_ODD, B, HGRN_HEADS, HGRN_DK, HGRN_DV), state_hgrn.dtype)
    y_prompt, conv_p, pool_p, hgrn_p = trunk(
        x_prompt, zero_conv, zero_pool, zero_hgrn, 0, ab_w_in, ab_w_out, conv_w, conv_b, conv_ln_g,
        conv_ln_b, pool_w, pool_scale, hgrn_w_in, hgrn_w_out, hgrn_gnorm, hgrn_lb, ffn_w_in, ffn_w_out, norm_g)
    y_sample, conv_s, pool_s, hgrn_s = trunk(
        x_sample, cache_conv, cache_pool, state_hgrn, PAST_LEN, ab_w_in, ab_w_out, conv_w, conv_b, conv_ln_g,
        conv_ln_b, pool_w, pool_scale, hgrn_w_in, hgrn_w_out, hgrn_gnorm, hgrn_lb, ffn_w_in, ffn_w_out, norm_g)
    return (y_prompt, y_sample, conv_p, pool_p, hgrn_p, conv_s, pool_s, hgrn_s)
```

```python
import numpy as np
from contextlib import ExitStack, contextmanager
import concourse.bass as bass
import concourse.mybir as mybir
from concourse.bass_utils import run_bass_kernel_spmd

F32 = mybir.dt.float32
BF16 = mybir.dt.bfloat16
ALU = mybir.AluOpType
AF = mybir.ActivationFunctionType

D = 2048
NCH = 16
DFF = 5632
NFF = 44
TP = 512
TS = 16
T = TP + TS
HALF = T // 2
NTILE = 4
EPS = 1e-6
R_NG, R_CW, R_CB, R_LG, R_LB, R_PS, R_GN, R_HL, R_END = 0, 192, 440, 448, 456, 464, 472, 473, 505
SLOT_ELEMS = 16 * 512


class Buf:
    def __init__(self, rel=None):
        self.w = None
        self.r = dict(rel) if rel else {}


class Q:
    def __init__(self, name):
        self.name = name
        self.ops = []
        self.count = 0
        self.seen = {}


class Prog:
    def __init__(self, nc):
        self.nc = nc
        self.q = {n: Q(n) for n in ("pe", "act", "dve", "pool", "sp")}
        self.dma_cnt = {}
        self.release = {}
        self.n_inst = 0

    def buf(self):
        return Buf(self.release)

    def bufs(self, n):
        return [Buf(self.release) for _ in range(n)]

    def deps_of(self, bufs):
        out = {}
        for b in bufs:
            if b.w and out.get(b.w[0], 0) < b.w[1]:
                out[b.w[0]] = b.w[1]
            for k, v in b.r.items():
                if out.get(k, 0) < v:
                    out[k] = v
        return list(out.items())

    def _waits(self, q, reads, writes, extra):
        deps = {}

        def add(k, v):
            if deps.get(k, 0) < v:
                deps[k] = v
        for b in reads:
            if b.w:
                add(*b.w)
        for b in writes:
            if b.w:
                add(*b.w)
            for k, v in b.r.items():
                add(k, v)
        for t in extra:
            if t:
                add(*t)
        waits = []
        for k, v in deps.items():
            if q.seen.get(k, 0) < v:
                q.seen[k] = v
                waits.append((k, v))
        return waits

    def op(self, qn, fns, reads=(), writes=(), extra=(), guard=()):
        q = self.q[qn]
        if callable(fns):
            fns = [fns]
        waits = self._waits(q, reads, list(writes) + list(guard), extra)
        q.count += 1
        tok = (qn, q.count)
        q.ops.append((waits, fns, None))
        self.n_inst += len(fns)
        for b in list(reads) + list(guard):
            b.r[qn] = q.count
        for b in writes:
            b.w = tok
            b.r = {}
        return tok

    def dma(self, qn, fn, key, reads=(), writes=(), extra=()):
        q = self.q[qn]
        waits = self._waits(q, reads, writes, extra)
        k = "d:" + key
        self.dma_cnt[k] = self.dma_cnt.get(k, 0) + 16
        tok = (k, self.dma_cnt[k])
        q.ops.append((waits, [fn], k))
        self.n_inst += 1
        for b in reads:
            b.r[k] = tok[1]
        for b in writes:
            b.w = tok
            b.r = {}
        return tok

    def clock(self):
        c = {n: q.count for n, q in self.q.items() if n in ("pe", "act", "dve", "pool")}
        for k, v in self.dma_cnt.items():
            if not k.startswith("d:w"):
                c[k] = v
        return c

    @contextmanager
    def scope(self):
        es = ExitStack()
        sc = Scope(self, es)
        try:
            yield sc
        finally:
            es.close()
            self.release = self.clock()


class Scope:
    def __init__(self, P, es):
        self.P = P
        self.es = es
        self.n = 0

    def sb(self, name, shape, dt):
        self.P.uid = getattr(self.P, "uid", 0) + 1
        return self.es.enter_context(self.P.nc.sbuf_tensor("%s_u%d" % (name, self.P.uid), shape, dt))


DBG = {'ffn_phase': 9, 'nsum': 1, 'sq': 1, 'copy': 1, 'kmax': 99}


def build(ntiles=NTILE, stop_after=99):
    nc = bass.Bass("TRN2", target_bir_lowering=False)
    P = Prog(nc)

    def din(name, shape):
        return nc.dram_tensor(name, shape, F32, kind="ExternalInput").ap()

    def dout(name, shape):
        return nc.dram_tensor(name, shape, F32, kind="ExternalOutput").ap()

    xp = din("xp", [2048, D])
    xs = din("xs", [4, TS, D])
    cconv = din("cconv", [4, 30, 1024])
    cpool = din("cpool", [4, 15, 1024])
    shg = din("shg", [4, 16, 128, 128])
    ab_w_in = din("ab_w_in", [1, D, 3072])
    ab_w_out = din("ab_w_out", [1, D, D])
    conv_w = din("conv_w", [1, 31, 1024])
    conv_b = din("conv_b", [1, 1024])
    conv_ln_g = din("conv_ln_g", [1, 1024])
    conv_ln_b = din("conv_ln_b", [1, 1024])
    pool_w = din("pool_w", [1, 4, 256, 256])
    pool_scale = din("pool_scale", [1, 1024])
    hgrn_w_in = din("hgrn_w_in", [1, D, 4 * D])
    hgrn_w_out = din("hgrn_w_out", [1, D, D])
    hgrn_gnorm = din("hgrn_gnorm", [1, 128])
    hgrn_lb = din("hgrn_lb", [2, D])
    ffn_w_in = din("ffn_w_in", [2, 2, D, 2 * DFF])
    ffn_w_out = din("ffn_w_out", [2, 2, DFF, D])
    norm_g = din("norm_g", [2, 6, D])

    yp = dout("yp", [2048, D])
    ys = dout("ys", [4, TS, D])
    o_conv_p = dout("o_conv_p", [30, 1024])
    o_pool_p = dout("o_pool_p", [15, 1024])
    o_hgrn_p = dout("o_hgrn_p", [16, 128, 128])
    o_conv_s = dout("o_conv_s", [4, 30, 1024])
    o_pool_s = dout("o_pool_s", [4, 15, 1024])
    o_hgrn_s = dout("o_hgrn_s", [4, 16, 128, 128])

    top = ExitStack()

    def sbp(name, shape, dt):
        return top.enter_context(nc.sbuf_tensor(name, shape, dt))

    xT = sbp("xT", [128, NCH, T], F32)
    xT_b = P.bufs(NCH)
    Sp = sbp("Sp", [128, 16, 128], F32)
    Sp_bf = sbp("Sp_bf", [128, 16, 128], BF16)
    Sp_b = P.bufs(16)
    Spbf_b = P.bufs(16)
    NSLOT = 3
    wslot = [sbp("wslot%d" % i, [128, SLOT_ELEMS], BF16) for i in range(NSLOT)]
    wslot_b = [P.bufs(4) for _ in range(NSLOT)]
    paramT = sbp("paramT", [128, 512], F32)
    par_b = P.buf()
    ones_bf = sbp("ones_bf", [128, 128], BF16)
    ones_f = sbp("ones_f", [128, 128], F32)
    ident_f = sbp("ident_f", [128, 128], F32)
    ident_bf = sbp("ident_bf", [128, 128], BF16)
    tmask = sbp("tmask", [64, 64], F32)
    rmask = sbp("rmask", [128, T], F32)
    icnt = sbp("icnt", [128, 16], F32)
    lbv = sbp("lbv", [128, 3, 16], F32)
    chist = sbp("chist", [128, 8, 30], F32)
    phist = sbp("phist", [128, 8, 15], F32)
    chist_b = P.buf()
    phist_b = P.buf()
    const_b = P.buf()
    ps = top.enter_context(nc.psum_tensor("ps", [128, 8, 512], F32))
    bank_b = P.bufs(8)
    reg_b = [P.bufs(4) for _ in range(8)]
    state = {"pair": 0, "bank": 0, "slot": 0}

    def next_pair():
        p = state["pair"]
        state["pair"] = (p + 1) % 4
        return p

    def next_bank():
        b = state["bank"]
        state["bank"] = (b + 1) % 8
        return b

    def bank_all(b):
        return [bank_b[b]]

    def pair_bufs(p):
        return bank_all(2 * p) + bank_all(2 * p + 1)

    def pview(p):
        return ps[:, 2 * p:2 * p + 2, 0:HALF]

    def sview(ap2d):
        return ap2d.rearrange("p (h c) -> p h c", h=2)

    def chain(qn, fns, reads, writes):
        tok = None
        for f in fns:
            tok = P.op(qn, f, reads=reads, writes=writes)
        return tok

    def init_consts():
        ms = [lambda e: e.memset(ones_f[:], 1.0), lambda e: e.memset(ones_bf[:], 1.0), lambda e: e.memset(rmask[:], 1.0),
              lambda e: e.memset(Sp[:], 0.0), lambda e: e.memset(Sp_bf[:], 0.0), lambda e: e.memset(chist[:], 0.0),
              lambda e: e.memset(phist[:], 0.0)]
        ms += [(lambda e, t=t: e.memset(icnt[:, t:t + 1], 1.0 / (t + 1))) for t in range(16)]
        P.op("dve", ms, writes=[const_b, chist_b, phist_b] + Sp_b + Spbf_b)
        P.op("dve", [lambda e: e.memset(rmask[:, 0:TP].rearrange("p (c k) -> p c k", k=64)[:, :, 0:1], 0.0),
                     lambda e: e.memset(rmask[:, TP:TP + 1], 0.0)], writes=[const_b])
        idb = P.buf()
        P.op("pool", lambda e: e.affine_select(out=ident_f[:], in_=ones_f[:], pattern=[[-1, 128]], compare_op=ALU.is_equal,
                                               fill=0.0, base=0, channel_multiplier=1), reads=[const_b], writes=[idb])
        P.op("pool", lambda e: e.affine_select(out=tmask[:], in_=ones_f[0:64, 0:64], pattern=[[1, 64]],
                                               compare_op=ALU.is_ge, fill=0.0, base=0, channel_multiplier=-1),
             reads=[const_b], writes=[par_b])
        P.op("act", lambda e: e.activation(out=ident_bf[:], in_=ident_f[:], func=AF.Copy), reads=[idb, par_b], writes=[const_b])
        with P.scope() as sc:
            stg = sc.sb("pstg", [128, 4, 128], F32)
            stg_b = P.buf()
            P.op("dve", lambda e: e.memset(stg[:], 0.0), writes=[stg_b])
            srcs = [(R_NG, norm_g.rearrange("l i (c p) -> (l i c) p", p=128), 192),
                    (R_CW, conv_w[0].rearrange("j (c p) -> (j c) p", p=128), 248),
                    (R_CB, conv_b[0].rearrange("(c p) -> c p", p=128), 8),
                    (R_LG, conv_ln_g[0].rearrange("(c p) -> c p", p=128), 8),
                    (R_LB, conv_ln_b[0].rearrange("(c p) -> c p", p=128), 8),
                    (R_PS, pool_scale[0].rearrange("(c p) -> c p", p=128), 8),
                    (R_GN, hgrn_gnorm, 1),
                    (R_HL, hgrn_lb.rearrange("l (c p) -> (l c) p", p=128), 32)]
            for r0, src, n in srcs:
                done = 0
                while done < n:
                    r = r0 + done
                    blk, off = r // 128, r % 128
                    cnt = min(n - done, 128 - off)
                    P.dma("sp", (lambda e, blk=blk, off=off, cnt=cnt, src=src, done=done:
                                 e.dma_start(out=stg[off:off + cnt, blk, :], in_=src[done:done + cnt, :])),
                          "par", writes=[stg_b])
                    done += cnt
            bk = next_bank()
            P.op("pe", [(lambda e, b=b: e.transpose(ps[:, bk, b * 128:(b + 1) * 128], stg[:, b, :], ident_f[:]))
                        for b in range(4)], reads=[stg_b, const_b, idb], writes=bank_all(bk))
            P.op("act", lambda e: e.activation(out=paramT[:], in_=ps[:, bk, :], func=AF.Copy),
                 reads=bank_all(bk), writes=[par_b])
            P.op("dve", lambda e: e.tensor_tensor(out=lbv[:, 1, :], in0=paramT[:, R_HL + 16:R_HL + 32],
                                                  in1=paramT[:, R_HL:R_HL + 16], op=ALU.subtract),
                 reads=[par_b], writes=[const_b])
            P.op("act", lambda e: e.activation(out=lbv[:, 0, :], in_=lbv[:, 1, :], func=AF.Sigmoid),
                 reads=[const_b], writes=[const_b])
            P.op("dve", lambda e: e.tensor_scalar(out=lbv[:, 1, :], in0=lbv[:, 0, :], scalar1=-1.0, scalar2=1.0,
                                                  op0=ALU.mult, op1=ALU.add), writes=[const_b])
            P.op("dve", lambda e: e.tensor_scalar(out=lbv[:, 2, :], in0=lbv[:, 0, :], scalar1=-1.0, scalar2=None, op0=ALU.add),
                 writes=[const_b, par_b])

    def pcol(r):
        return paramT[:, r:r + 1]

    def load_w(pieces, kc_n, ncols):
        s = state["slot"]
        state["slot"] = (s + 1) % NSLOT
        view = wslot[s][:, 0:kc_n * ncols].rearrange("p (k m) -> p k m", k=kc_n)
        deps = P.deps_of(wslot_b[s])
        for i, pc in enumerate(pieces):
            src, c0, n = pc[0], pc[1], pc[2]
            k0 = pc[3] if len(pc) > 3 else 0
            kn = src.shape[0] // 128
            P.dma("pool", (lambda e, src=src, c0=c0, n=n, view=view, k0=k0, kn=kn:
                           e.dma_start(out=view[:, k0:k0 + kn, c0:c0 + n], in_=src.rearrange("(k p) m -> p k m", p=128))),
                  "w%d" % s, writes=[wslot_b[s][i]], extra=deps)
        for b in wslot_b[s][len(pieces):]:
            b.w = None
            b.r = {}
        return wslot_b[s], view

    def mm_pair(pr, lhs_list, rhs_fn, reads):
        n = len(lhs_list)
        fns = []
        for k in range(n):
            for h in range(2):
                fns.append(lambda e, k=k, h=h: e.matmul(ps[:, 2 * pr + h, 0:HALF], lhs_list[k], rhs_fn(k, h),
                                                         start=(k == 0), stop=(k == n - 1)))
        return P.op("pe", fns, reads=reads, writes=pair_bufs(pr))

    def rhs_of(t3):
        return lambda k, h: t3[:, k, h * HALF:(h + 1) * HALF]

    def finish_rstd(pr, inv_n, rstd, rstd_b, ln_bias=0.0):
        P.op("act", lambda e: e.activation(out=sview(rstd[:]), in_=pview(pr), func=AF.Ln, scale=inv_n, bias=EPS),
             reads=pair_bufs(pr), writes=[rstd_b])
        if ln_bias != 0.0:
            P.op("act", lambda e: e.activation(out=rstd[:], in_=rstd[:], func=AF.Exp, scale=-0.5, bias=ln_bias), writes=[rstd_b])
        else:
            P.op("act", lambda e: e.activation(out=rstd[:], in_=rstd[:], func=AF.Exp, scale=-0.5), writes=[rstd_b])

    def prenorm(l, i, hT, hT_b, tmp):
        sqs, sq_b, rstd, rstd_b = tmp
        pr = next_pair()
        for c in range(NCH):
            ii = c % len(sqs)
            P.op("act", (lambda e, c=c, ii=ii: e.activation(out=sqs[ii][:], in_=xT[:, c, :], func=AF.Square)),
                 reads=[xT_b[c]], writes=[sq_b[ii]])
            P.op("pe", [(lambda e, ii=ii, h=h, c=c: e.matmul(ps[:, 2 * pr + h, 0:HALF], ones_bf[:],
                                                             sqs[ii][:, h * HALF:(h + 1) * HALF],
                                                             start=(c == 0), stop=(c == NCH - 1))) for h in range(2)],
                 reads=[sq_b[ii], const_b], writes=pair_bufs(pr))
        finish_rstd(pr, 1.0 / D, rstd, rstd_b)
        for c in range(NCH):
            P.op("dve", (lambda e, c=c: e.scalar_tensor_tensor(out=hT[:, c, :], in0=xT[:, c, :],
                                                               scalar=pcol(R_NG + (l * 6 + i) * 16 + c), in1=rstd[:],
                                                               op0=ALU.mult, op1=ALU.mult)),
                 reads=[xT_b[c], rstd_b, par_b], writes=[hT_b[c]])

    def postnorm_add(l, i, hout, hout_b, nsum_pr, alpha, tmp):
        sqs, sq_b, rstd, rstd_b = tmp
        finish_rstd(nsum_pr, 1.0 / D, rstd, rstd_b, ln_bias=float(np.log(alpha)) if alpha != 1.0 else 0.0)
        for c in range(NCH):
            P.op("dve", (lambda e, c=c: e.scalar_tensor_tensor(out=hout[:, c, :], in0=hout[:, c, :],
                                                               scalar=pcol(R_NG + (l * 6 + i) * 16 + c),
                                                               in1=rstd[:], op0=ALU.mult, op1=ALU.mult)),
                 reads=[rstd_b, par_b], writes=[hout_b[c]])
        for c in range(NCH):
            P.op("dve", (lambda e, c=c: e.tensor_tensor(out=xT[:, c, :], in0=xT[:, c, :], in1=hout[:, c, :], op=ALU.add)),
                 reads=[hout_b[c]], writes=[xT_b[c]])

    def out_proj(w2d, kc_n, rhs3, rhs_b, hout, hout_b, tmp, gcols):
        sqs, sq_b, rstd, rstd_b = tmp
        nsum = next_pair()
        mi = 0
        pending = None
        for g0 in range(0, D, gcols):
            if kc_n > 16:
                kq = kc_n // 4
                pcs = [(w2d[q * kq * 128:(q + 1) * kq * 128, g0:g0 + gcols], 0, gcols, q * kq) for q in range(4)]
            else:
                pcs = [(w2d[:, g0:g0 + gcols], 0, gcols)]
            wb, wv = load_w(pcs, kc_n, gcols)
            for mc in range(gcols // 128):
                pr = next_pair()
                if pr == nsum:
                    pr = next_pair()
                mm_pair(pr, [wv[:, k, mc * 128:(mc + 1) * 128] for k in range(min(kc_n, DBG['kmax']))], rhs_of(rhs3), wb + rhs_b)
                m = mi
                ii = m % len(sqs)
                if DBG['copy']:
                    P.op("dve", (lambda e, m=m, pr=pr: e.tensor_copy(out=sview(hout[:, m, :]), in_=pview(pr))),
                         reads=pair_bufs(pr), writes=[hout_b[m]])
                if DBG['sq']:
                    P.op("act", (lambda e, ii=ii, m=m: e.activation(out=sqs[ii][:], in_=hout[:, m, :], func=AF.Square)),
                         reads=[hout_b[m]], writes=[sq_b[ii]])
                if pending is not None:
                    pending()

                def pending(ii=ii, m=m):
                    P.op("pe", [(lambda e, ii=ii, h=h, m=m: e.matmul(ps[:, 2 * nsum + h, 0:HALF], ones_bf[:],
                                                                     sqs[ii][:, h * HALF:(h + 1) * HALF],
                                                                     start=(m == 0), stop=(m == NCH - 1))) for h in range(2)],
                         reads=[sq_b[ii], const_b], writes=pair_bufs(nsum))
                mi += 1
        pending()
        return nsum

    def mk_tmp_p(tag):
        sqs = [sbp("sq%s%d" % (tag, i), [128, T], BF16) for i in range(2)]
        sq_b = P.bufs(2)
        rstd = sbp("rstd" + tag, [128, T], F32)
        return (sqs, sq_b, rstd, P.buf())
    TMP_PRE = mk_tmp_p("pre")
    TMP_POST = mk_tmp_p("post")

    def ffn(l, j, tag):
        w_in = ffn_w_in[l, j]
        w_out = ffn_w_out[l, j]
        with P.scope() as sc:
            hT = sc.sb("hT" + tag, [128, NCH, T], BF16)
            hT_b = P.bufs(NCH)
            gT = sc.sb("gT" + tag, [128, NFF, T], BF16)
            gT_b = P.bufs(NFF)
            hout = sc.sb("hout" + tag, [128, NCH, T], F32)
            hout_b = P.bufs(NCH)
            sa = [sc.sb("sa%s%d" % (tag, i), [128, T], F32) for i in range(2)]
            sa_b = P.bufs(2)
            prenorm(l, 0 if j == 0 else 4, hT, hT_b, TMP_PRE)
            for grp in range(NFF // 2 if DBG['ffn_phase'] >= 1 else 0):
                c0 = grp * 256
                wb, wv = load_w([(w_in[:, c0:c0 + 256], 0, 256), (w_in[:, DFF + c0:DFF + c0 + 256], 256, 256)], NCH, 512)
                for jj in range(2):
                    fch = grp * 2 + jj
                    pa = next_pair()
                    pb = next_pair()
                    mm_pair(pa, [wv[:, k, jj * 128:(jj + 1) * 128] for k in range(NCH)], rhs_of(hT), wb + hT_b)
                    mm_pair(pb, [wv[:, k, 256 + jj * 128:256 + (jj + 1) * 128] for k in range(NCH)], rhs_of(hT), wb + hT_b)
                    ii = fch % 2
                    P.op("act", (lambda e, ii=ii, pa=pa: e.activation(out=sview(sa[ii][:]), in_=pview(pa), func=AF.Silu)),
                         reads=pair_bufs(pa), writes=[sa_b[ii]])
                    P.op("dve", (lambda e, ii=ii, pb=pb, fch=fch: e.tensor_tensor(out=sview(gT[:, fch, :]), in0=sview(sa[ii][:]),
                                                                                  in1=pview(pb), op=ALU.mult)),
                         reads=[sa_b[ii]] + pair_bufs(pb), writes=[gT_b[fch]])
            if DBG['ffn_phase'] >= 2:
                nsum = out_proj(w_out, NFF, gT, gT_b, hout, hout_b, TMP_POST, 128)
            if DBG['ffn_phase'] >= 3:
                postnorm_add(l, 1 if j == 0 else 5, hout, hout_b, nsum, 0.5, TMP_POST)

    def load_tile(t):
        with P.scope() as sc:
            stg = [sc.sb("xstg%d" % i, [128, D], F32) for i in range(2)]
            stg_b = P.bufs(2)
            for blk in range(5):
                i = blk % 2
                ntok = 128 if blk < 4 else TS
                src = xp[t * TP + blk * 128:t * TP + (blk + 1) * 128, :] if blk < 4 else xs[t]
                P.dma("sp", (lambda e, i=i, ntok=ntok, src=src: e.dma_start(out=stg[i][0:ntok, :], in_=src)),
                      "xin%d" % i, writes=[stg_b[i]])
                for cg in range(4):
                    bk = next_bank()
                    P.op("pe", [(lambda e, c=c, i=i, ntok=ntok, bk=bk, cg=cg:
                                 e.transpose(ps[:, bk, (c - cg * 4) * 128:(c - cg * 4) * 128 + ntok],
                                             stg[i][0:ntok, c * 128:(c + 1) * 128], ident_f[0:ntok, 0:ntok]))
                                for c in range(cg * 4, cg * 4 + 4)],
                         reads=[stg_b[i], const_b], writes=bank_all(bk))
                    P.op("act", (lambda e, bk=bk, cg=cg, blk=blk, ntok=ntok:
                                 e.activation(out=xT[:, cg * 4:cg * 4 + 4, blk * 128:blk * 128 + ntok],
                                              in_=ps[:, bk, :].rearrange("p (c k) -> p c k", k=128)[:, :, 0:ntok],
                                              func=AF.Copy)),
                         reads=bank_all(bk), guard=xT_b[cg * 4:cg * 4 + 4])
            for c in range(NCH):
                xT_b[c].w = ("act", P.q["act"].count)
                xT_b[c].r = {}

    def store_tile(t):
        with P.scope() as sc:
            stg = [sc.sb("ystg%d" % i, [128, D], F32) for i in range(2)]
            stg_b = [P.bufs(4) for _ in range(2)]
            for blk in range(5):
                i = blk % 2
                ntok = 128 if blk < 4 else TS
                dst = yp[t * TP + blk * 128:t * TP + (blk + 1) * 128, :] if blk < 4 else ys[t]
                for cg in range(4):
                    bk = next_bank()
                    P.op("pe", [(lambda e, c=c, ntok=ntok, bk=bk, cg=cg, blk=blk:
                                 e.transpose(ps[0:ntok, bk, (c - cg * 4) * 128:(c - cg * 4 + 1) * 128],
                                             xT[:, c, blk * 128:blk * 128 + ntok], ident_f[:]))
                                for c in range(cg * 4, cg * 4 + 4)],
                         reads=xT_b[cg * 4:cg * 4 + 4] + [const_b], writes=bank_all(bk))
                    P.op("act", (lambda e, bk=bk, cg=cg, i=i, ntok=ntok:
                                 e.activation(out=stg[i][0:ntok, cg * 512:(cg + 1) * 512], in_=ps[0:ntok, bk, :], func=AF.Copy)),
                         reads=bank_all(bk), writes=[stg_b[i][cg]])
                P.dma("sp", (lambda e, i=i, ntok=ntok, dst=dst: e.dma_start(out=dst, in_=stg[i][0:ntok, :])),
                      "yout%d" % i, reads=stg_b[i])

    def mixer_ab(t):
        l = 0
        last = (t == ntiles - 1)
        AW = 30 + TP + 30 + TS
        PW = 15 + TP + 15 + TS
        CW = AW - 30
        CH = CW // 2
        with P.scope() as sc0:
            m_in = sc0.sb("m_in", [128, NCH, T], BF16)
            m_in_b = P.bufs(NCH)
            with P.scope() as scA:
                abuf = scA.sb("abuf", [128, 8, AW], F32)
                abuf_b = P.bufs(8)
                pbuf = scA.sb("pbuf", [128, 8, PW], F32)
                pbuf_b = P.bufs(8)
                with P.scope() as sc:
                    stg4 = sc.sb("stg4", [32, 2, 1024], F32)
                    stg4_b = [P.bufs(2) for _ in range(2)]
                    hT = sc.sb("hTab", [128, NCH, T], BF16)
                    hT_b = P.bufs(NCH)
                    sg = [sc.sb("sgab%d" % i, [128, T], F32) for i in range(2)]
                    sg_b = P.bufs(2)
                    prenorm(l, 2, hT, hT_b, TMP_PRE)
                    P.op("dve", lambda e: e.tensor_copy(out=abuf[:, :, 0:30], in_=chist[:]), reads=[chist_b], guard=abuf_b)
                    P.op("dve", lambda e: e.tensor_copy(out=pbuf[:, :, 0:15], in_=phist[:]), reads=[phist_b], guard=pbuf_b)
                    P.dma("sp", lambda e: e.dma_start(out=stg4[0:30, 0, :], in_=cconv[t]), "hin0", writes=stg4_b[0])
                    P.dma("sp", lambda e: e.dma_start(out=stg4[0:15, 1, :], in_=cpool[t]), "hin1", writes=stg4_b[1])
                    for which, n, dstbuf, dst_b, off in ((0, 30, abuf, abuf_b, 30 + TP), (1, 15, pbuf, pbuf_b, 15 + TP)):
                        for cg in range(2):
                            bk = next_bank()
                            P.op("pe", [(lambda e, c=c, bk=bk, cg=cg, which=which, n=n:
                                         e.transpose(ps[:, bk, (c - cg * 4) * 32:(c - cg * 4) * 32 + n],
                                                     stg4[0:n, which, c * 128:(c + 1) * 128], ident_f[0:n, 0:n]))
                                        for c in range(cg * 4, cg * 4 + 4)],
                                 reads=stg4_b[which] + [const_b], writes=bank_all(bk))
                            P.op("act", (lambda e, bk=bk, cg=cg, n=n, dstbuf=dstbuf, off=off:
                                         e.activation(out=dstbuf[:, cg * 4:cg * 4 + 4, off:off + n],
                                                      in_=ps[:, bk, 0:128].rearrange("p (c k) -> p c k", k=32)[:, :, 0:n],
                                                      func=AF.Copy)),
                                 reads=bank_all(bk), guard=dst_b[cg * 4:cg * 4 + 4])
                    w_in = ab_w_in[0]
                    for grp in range(4):
                        c0 = grp * 256
                        wb, wv = load_w([(w_in[:, c0:c0 + 256], 0, 256), (w_in[:, 1024 + c0:1024 + c0 + 256], 256, 256)], NCH, 512)
                        for jj in range(2):
                            ch = grp * 2 + jj
                            pa = next_pair()
                            pb = next_pair()
                            mm_pair(pa, [wv[:, k, jj * 128:(jj + 1) * 128] for k in range(NCH)], rhs_of(hT), wb + hT_b)
                            mm_pair(pb, [wv[:, k, 256 + jj * 128:256 + (jj + 1) * 128] for k in range(NCH)], rhs_of(hT), wb + hT_b)
                            ii = ch % 2
                            P.op("act", (lambda e, ii=ii, pb=pb: e.activation(out=sview(sg[ii][:]), in_=pview(pb), func=AF.Sigmoid)),
                                 reads=pair_bufs(pb), writes=[sg_b[ii]])
                            fa = [
                                (lambda e, ii=ii, pa=pa, ch=ch: e.tensor_tensor(out=abuf[:, ch, 30:30 + HALF], in0=sg[ii][:, 0:HALF],
                                                                                in1=ps[:, 2 * pa, 0:HALF], op=ALU.mult)),
                                (lambda e, ii=ii, pa=pa, ch=ch: e.tensor_tensor(out=abuf[:, ch, 30 + HALF:30 + TP], in0=sg[ii][:, HALF:TP],
                                                                                in1=ps[:, 2 * pa + 1, 0:TP - HALF], op=ALU.mult)),
                                (lambda e, ii=ii, pa=pa, ch=ch: e.tensor_tensor(out=abuf[:, ch, 60 + TP:60 + T], in0=sg[ii][:, TP:T],
                                                                                in1=ps[:, 2 * pa + 1, TP - HALF:HALF], op=ALU.mult))]
                            P.op("dve", fa, reads=[sg_b[ii]] + pair_bufs(pa), guard=[abuf_b[ch]])
                    for grp in range(2):
                        c0 = 2048 + grp * 512
                        wb, wv = load_w([(w_in[:, c0:c0 + 512], 0, 512)], NCH, 512)
                        for jj in range(4):
                            ch = grp * 4 + jj
                            pa = next_pair()
                            mm_pair(pa, [wv[:, k, jj * 128:(jj + 1) * 128] for k in range(NCH)], rhs_of(hT), wb + hT_b)
                            fp = [
                                (lambda e, pa=pa, ch=ch: e.activation(out=pbuf[:, ch, 15:15 + HALF], in_=ps[:, 2 * pa, 0:HALF], func=AF.Copy)),
                                (lambda e, pa=pa, ch=ch: e.activation(out=pbuf[:, ch, 15 + HALF:15 + TP], in_=ps[:, 2 * pa + 1, 0:TP - HALF],
                                                                      func=AF.Copy)),
                                (lambda e, pa=pa, ch=ch: e.activation(out=pbuf[:, ch, 30 + TP:30 + T], in_=ps[:, 2 * pa + 1, TP - HALF:HALF],
                                                                      func=AF.Copy))]
                            P.op("act", fp, reads=pair_bufs(pa), guard=[pbuf_b[ch]])
                    P.op("dve", lambda e: e.tensor_copy(out=chist[:], in_=abuf[:, :, TP:TP + 30]), writes=[chist_b] + abuf_b)
                    P.op("dve", lambda e: e.tensor_copy(out=phist[:], in_=pbuf[:, :, TP:TP + 15]), writes=[phist_b] + pbuf_b)
                with P.scope() as sco:
                    stg4 = sco.sb("stg4o", [32, 2, 1024], F32)
                    stg4_b = [P.bufs(2) for _ in range(2)]
                    outs = [(abuf, abuf_b, 30 + TP + 16, 30, o_conv_s[t], 0), (pbuf, pbuf_b, 15 + TP + 16, 15, o_pool_s[t], 1)]
                    if last:
                        outs += [(abuf, abuf_b, TP, 30, o_conv_p, 0), (pbuf, pbuf_b, TP, 15, o_pool_p, 1)]
                    for srcbuf, src_b, off, n, dst, oi in outs:
                        for cg in range(2):
                            bk = next_bank()
                            P.op("pe", [(lambda e, c=c, bk=bk, cg=cg, srcbuf=srcbuf, off=off, n=n:
                                         e.transpose(ps[0:n, bk, (c - cg * 4) * 128:(c - cg * 4 + 1) * 128],
                                                     srcbuf[:, c, off:off + n], ident_f[:]))
                                        for c in range(cg * 4, cg * 4 + 4)],
                                 reads=src_b[cg * 4:cg * 4 + 4] + [const_b], writes=bank_all(bk))
                            P.op("act", (lambda e, bk=bk, cg=cg, n=n, oi=oi:
                                         e.activation(out=stg4[0:n, oi, cg * 512:(cg + 1) * 512], in_=ps[0:n, bk, :], func=AF.Copy)),
                                 reads=bank_all(bk), writes=[stg4_b[oi][cg]])
                        P.dma("sp", (lambda e, n=n, oi=oi, dst=dst: e.dma_start(out=dst, in_=stg4[0:n, oi, :])),
                              "cout%d" % oi, reads=stg4_b[oi])
                with P.scope() as sc:
                    pt = [sc.sb("ptmp%d" % i, [128, 2, PW], F32) for i in range(2)]
                    pt_b = P.bufs(2)
                    pd = sc.sb("pdiff", [128, 8, T], BF16)
                    pd_b = P.bufs(8)
                    for gi in range(4):
                        win = 2 << gi
                        cs = slice(2 * gi, 2 * gi + 2)
                        bb = [pt[0][:, :, :], pt[1][:, :, :]]
                        gb = pt_b + pd_b[2 * gi:2 * gi + 2]
                        rb = pbuf_b[2 * gi:2 * gi + 2] + [const_b]
                        fl = [lambda e, bb=bb, cs=cs: e.tensor_tensor(out=bb[0][:, :, 1:PW], in0=pbuf[:, cs, 1:PW],
                                                                      in1=pbuf[:, cs, 0:PW - 1], op=ALU.add)]
                        cur, k, lo = 0, 2, 1
                        while k < win:
                            nxt = 1 - cur
                            fl.append(lambda e, bb=bb, cur=cur, nxt=nxt, lo=lo, k=k:
                                      e.tensor_tensor(out=bb[nxt][:, :, lo + k:PW], in0=bb[cur][:, :, lo + k:PW],
                                                      in1=bb[cur][:, :, lo:PW - k], op=ALU.add))
                            lo += k
                            k *= 2
                            cur = nxt
                        res = bb[cur]
                        fl.append(lambda e, res=res, cs=cs, win=win:
                                  e.scalar_tensor_tensor(out=pd[:, cs, 0:TP], in0=res[:, :, 15:15 + TP], scalar=1.0 / win,
                                                         in1=pbuf[:, cs, 15:15 + TP], op0=ALU.mult, op1=ALU.subtract))
                        fl.append(lambda e, res=res, cs=cs, win=win:
                                  e.scalar_tensor_tensor(out=pd[:, cs, TP:T], in0=res[:, :, 30 + TP:30 + T], scalar=1.0 / win,
                                                         in1=pbuf[:, cs, 30 + TP:30 + T], op0=ALU.mult, op1=ALU.subtract))
                        if t == 0:
                            for kk in range(2):
                                cc = 2 * gi + kk
                                fl.append(lambda e, res=res, kk=kk, win=win:
                                          e.tensor_tensor(out=res[:, kk, 15:15 + win - 1], in0=res[:, kk, 15:15 + win - 1],
                                                          in1=icnt[:, 0:win - 1], op=ALU.mult))
                                fl.append(lambda e, res=res, kk=kk, cc=cc, win=win:
                                          e.tensor_tensor(out=pd[:, cc, 0:win - 1], in0=res[:, kk, 15:15 + win - 1],
                                                          in1=pbuf[:, cc, 15:15 + win - 1], op=ALU.subtract))
                        chain("dve", fl, rb, gb)
                    wb, wv = load_w([(pool_w[0, g_], g_ * 256, 256) for g_ in range(4)], 2, 1024)
                    for dch in range(8):
                        gi = dch // 2
                        pa = next_pair()
                        mm_pair(pa, [wv[:, k, gi * 256 + (dch % 2) * 128:gi * 256 + (dch % 2 + 1) * 128] for k in range(2)],
                                (lambda k, h, gi=gi: pd[:, 2 * gi + k, h * HALF:(h + 1) * HALF]), wb + pd_b[2 * gi:2 * gi + 2])
                        P.op("act", (lambda e, pa=pa, dch=dch: e.activation(out=sview(m_in[:, 8 + dch, :]), in_=pview(pa), func=AF.Copy,
                                                                            scale=pcol(R_PS + dch))),
                             reads=pair_bufs(pa) + [par_b], writes=[m_in_b[8 + dch]])
                with P.scope() as sc:
                    ybuf = sc.sb("ybuf", [128, 8, CW], F32)
                    ybuf_b = P.bufs(8)
                    for c in range(8):
                        P.op("dve", (lambda e, c=c: e.tensor_scalar(out=ybuf[:, c, :], in0=abuf[:, c, 0:CW], scalar1=pcol(R_CW + c),
                                                                    scalar2=pcol(R_CB + c), op0=ALU.mult, op1=ALU.add)),
                             reads=[abuf_b[c], par_b], writes=[ybuf_b[c]])
                    JS = 24
                    ybuf2 = sc.sb("ybuf2", [128, 8, CW], F32)
                    ybuf2_b = P.bufs(8)
                    ptm = [sc.sb("ptm%d" % i, [128, CW], F32) for i in range(1)]
                    ptm_b = P.bufs(1)
                    pi_ = 0
                    for c in range(8):
                        P.op("pool", (lambda e, c=c: e.tensor_scalar(out=ybuf2[:, c, :], in0=abuf[:, c, JS:JS + CW],
                                                                     scalar1=pcol(R_CW + JS * 8 + c), scalar2=None, op0=ALU.mult)),
                             reads=[abuf_b[c], par_b], writes=[ybuf2_b[c]])
                    for j in range(1, 31):
                        for c in range(8):
                            if j < JS:
                                P.op("dve", (lambda e, c=c, j=j: e.scalar_tensor_tensor(out=ybuf[:, c, :], in0=abuf[:, c, j:j + CW],
                                                                                        scalar=pcol(R_CW + j * 8 + c), in1=ybuf[:, c, :],
                                                                                        op0=ALU.mult, op1=ALU.add)),
                                     reads=[abuf_b[c]], writes=[ybuf_b[c]])
                            elif j > JS:
                                pp = 0
                                pi_ += 1
                                P.op("pool", (lambda e, c=c, j=j, pp=pp: e.tensor_scalar(out=ptm[pp][:], in0=abuf[:, c, j:j + CW],
                                                                                         scalar1=pcol(R_CW + j * 8 + c), scalar2=None,
                                                                                         op0=ALU.mult)),
                                     reads=[abuf_b[c], par_b], writes=[ptm_b[pp]])
                                P.op("pool", (lambda e, c=c, pp=pp: e.tensor_tensor(out=ybuf2[:, c, :], in0=ybuf2[:, c, :], in1=ptm[pp][:],
                                                                                    op=ALU.add)),
                                     reads=[ptm_b[pp]], writes=[ybuf2_b[c]])
                    for c in range(8):
                        P.op("dve", (lambda e, c=c: e.tensor_tensor(out=ybuf[:, c, :], in0=ybuf[:, c, :], in1=ybuf2[:, c, :], op=ALU.add)),
                             reads=[ybuf2_b[c]], writes=[ybuf_b[c]])
                    ysq = sc.sb("ysq", [128, 2, CW], F32)
                    ysq_b = P.bufs(2)
                    p1 = next_pair()
                    p2 = next_pair()
                    for c in range(8):
                        ii = c % 2
                        P.op("dve", (lambda e, c=c, ii=ii: e.tensor_tensor(out=ysq[:, ii, :], in0=ybuf[:, c, :], in1=ybuf[:, c, :], op=ALU.mult)),
                             reads=[ybuf_b[c]], writes=[ysq_b[ii]])
                        P.op("pe", [(lambda e, c=c, h=h: e.matmul(ps[:, 2 * p1 + h, 0:CH], ones_f[:], ybuf[:, c, h * CH:(h + 1) * CH],
                                                                  start=(c == 0), stop=(c == 7))) for h in range(2)],
                             reads=[ybuf_b[c], const_b], writes=pair_bufs(p1))
                        P.op("pe", [(lambda e, c=c, h=h, ii=ii: e.matmul(ps[:, 2 * p2 + h, 0:CH], ones_f[:], ysq[:, ii, h * CH:(h + 1) * CH],
                                                                         start=(c == 0), stop=(c == 7))) for h in range(2)],
                             reads=[ysq_b[ii], const_b], writes=pair_bufs(p2))
                    mean = sc.sb("lnmean", [128, CW], F32)
                    lrstd = sc.sb("lnrstd", [128, CW], F32)
                    mean_b = P.buf()
                    ln_b = P.buf()

                    def cview(ap2d):
                        return ap2d.rearrange("p (h c) -> p h c", h=2)
                    P.op("dve", lambda e: e.tensor_scalar(out=cview(mean[:]), in0=ps[:, 2 * p1:2 * p1 + 2, 0:CH], scalar1=1.0 / 1024,
                                                          scalar2=None, op0=ALU.mult),
                         reads=pair_bufs(p1), writes=[mean_b])
                    P.op("dve", lambda e: e.tensor_tensor(out=lrstd[:], in0=mean[:], in1=mean[:], op=ALU.mult), reads=[mean_b], writes=[ln_b])
                    P.op("dve", lambda e: e.scalar_tensor_tensor(out=cview(lrstd[:]), in0=ps[:, 2 * p2:2 * p2 + 2, 0:CH], scalar=1.0 / 1024,
                                                                 in1=cview(lrstd[:]), op0=ALU.mult, op1=ALU.subtract),
                         reads=pair_bufs(p2), writes=[ln_b])
                    P.op("dve", lambda e: e.tensor_scalar(out=lrstd[:], in0=lrstd[:], scalar1=0.0, scalar2=None, op0=ALU.max), writes=[ln_b])
                    P.op("act", lambda e: e.activation(out=lrstd[:], in_=lrstd[:], func=AF.Ln, bias=EPS), writes=[ln_b])
                    P.op("act", lambda e: e.activation(out=lrstd[:], in_=lrstd[:], func=AF.Exp, scale=-0.5), writes=[ln_b])
                    for c in range(8):
                        P.op("dve", (lambda e, c=c: e.tensor_tensor(out=ybuf[:, c, :], in0=ybuf[:, c, :], in1=mean[:], op=ALU.subtract)),
                             reads=[mean_b], writes=[ybuf_b[c]])
                    for c in range(8):
                        P.op("dve", (lambda e, c=c: e.tensor_tensor(out=ybuf[:, c, :], in0=ybuf[:, c, :], in1=lrstd[:], op=ALU.mult)),
                             reads=[ln_b], writes=[ybuf_b[c]])
                        P.op("act", [(lambda e, c=c: e.activation(out=m_in[:, c, 0:TP], in_=ybuf[:, c, 0:TP], func=AF.Silu,
                                                                  scale=pcol(R_LG + c), bias=pcol(R_LB + c))),
                                     (lambda e, c=c: e.activation(out=m_in[:, c, TP:T], in_=ybuf[:, c, 30 + TP:30 + T], func=AF.Silu,
                                                                  scale=pcol(R_LG + c), bias=pcol(R_LB + c)))],
                             reads=[ybuf_b[c], par_b], writes=[m_in_b[c]])
            with P.scope() as sc:
                hout = sc.sb("houtab", [128, NCH, T], F32)
                hout_b = P.bufs(NCH)
                nsum = out_proj(ab_w_out[0], NCH, m_in, m_in_b, hout, hout_b, TMP_POST, 512)
                postnorm_add(l, 3, hout, hout_b, nsum, 1.0, TMP_POST)

    def mixer_hgrn(t):
        l = 1
        last = (t == ntiles - 1)
        w_in = hgrn_w_in[0]
        chunks = [(ci * 64, 64) for ci in range(8)] + [(TP, TS)]
        with P.scope() as sc0:
            onT = sc0.sb("onT", [128, NCH, T], BF16)
            onT_b = P.bufs(NCH)
            with P.scope() as sc:
                hT = sc.sb("hThg", [128, NCH, T], BF16)
                hT_b = P.bufs(NCH)
                Ss = sc.sb("Ss", [128, 16, 128], F32)
                Ss_bf = sc.sb("Ss_bf", [128, 16, 128], BF16)
                Ss_b = P.bufs(16)
                Ssbf_b = P.bufs(16)
                P.dma("sp", lambda e: e.dma_start(out=Ss[:], in_=shg[t].rearrange("h k v -> k h v")), "sin", writes=Ss_b)
                P.op("act", lambda e: e.activation(out=Ss_bf[:], in_=Ss[:], func=AF.Copy), reads=Ss_b, writes=Ssbf_b)
                qts = [sc.sb("qt%d" % i, [128, 4, T], BF16) for i in range(2)]
                qt_bs = [P.bufs(4) for _ in range(2)]
                kt = sc.sb("kt", [128, 4, T], BF16)
                sgz = sc.sb("sgz", [128, 4, T], BF16)
                ebl = sc.sb("ebl", [128, 4, 16], F32)
                vtok = sc.sb("vtok", [64, 9, 512], BF16)
                oT = sc.sb("oT", [128, 4, T], F32)
                kt_b, sgz_b, ebl_b, oT_b = P.bufs(4), P.bufs(4), P.bufs(4), P.bufs(4)
                vtok_b = P.bufs(9)
                tm = [sc.sb("hgt%d" % i, [128, T], F32) for i in range(7)]
                tm_b = P.bufs(7)
                tm0x = [tm[0], sc.sb("hgt0b", [128, T], F32)]
                tm0x_b = [tm_b[0], P.buf()]
                khat = [sc.sb("khat%d" % i, [128, 64], BF16) for i in range(3)]
                khat_b = P.bufs(3)
                asb = [sc.sb("asb%d" % i, [64, 64], BF16) for i in range(3)]
                asb_b = P.bufs(3)
                ktk = [sc.sb("ktk%d" % i, [64, 128], BF16) for i in range(3)]
                ktk_b = P.bufs(3)
                prenorm(l, 2, hT, hT_b, TMP_PRE)
                def q_thunks(g, qt, qt_b):
                    wb, wq = load_w([(w_in[:, g * 512:(g + 1) * 512], 0, 512)], NCH, 512)

                    def mk(hl):
                        def th():
                            pr = next_pair()
                            mm_pair(pr, [wq[:, k, hl * 128:(hl + 1) * 128] for k in range(NCH)], rhs_of(hT), wb + hT_b)
                            P.op("act", (lambda e, pr=pr, hl=hl: e.activation(out=sview(qt[:, hl, :]), in_=pview(pr), func=AF.Silu)),
                                 reads=pair_bufs(pr), writes=[qt_b[hl]])
                        return th
                    return [mk(hl) for hl in range(4)]

                def gz_thunks(g):
                    wb, wg = load_w([(w_in[:, 3 * D + g * 512:3 * D + (g + 1) * 512], 0, 512)], NCH, 512)

                    def mk(hl):
                        def th():
                            pr = next_pair()
                            mm_pair(pr, [wg[:, k, hl * 128:(hl + 1) * 128] for k in range(NCH)], rhs_of(hT), wb + hT_b)
                            P.op("act", (lambda e, pr=pr, hl=hl: e.activation(out=sview(sgz[:, hl, :]), in_=pview(pr), func=AF.Silu)),
                                 reads=pair_bufs(pr), writes=[sgz_b[hl]])
                        return th
                    return [mk(hl) for hl in range(4)]

                def do_group(g, qt, qt_b, qt_n, qt_nb):
                    wb, wf = load_w([(w_in[:, D + g * 512:D + (g + 1) * 512], 0, 512)], NCH, 512)
                    for hl in range(4):
                        h = 4 * g + hl
                        pr = next_pair()
                        mm_pair(pr, [wf[:, k, hl * 128:(hl + 1) * 128] for k in range(NCH)], rhs_of(hT), wb + hT_b)
                        t0 = tm0x[hl % 2]
                        t0_b = tm0x_b[hl % 2]
                        P.op("act", (lambda e, pr=pr, t0=t0: e.activation(out=sview(t0[:]), in_=pview(pr), func=AF.Sigmoid, scale=-1.0)),
                             reads=pair_bufs(pr), writes=[t0_b])
                        P.op("dve", (lambda e, h=h, t0=t0: e.tensor_scalar(out=tm[1][:], in0=t0[:], scalar1=lbv[:, 2, h:h + 1], scalar2=1.0,
                                                                           op0=ALU.mult, op1=ALU.add)),
                             reads=[t0_b, const_b], writes=[tm_b[1]])
                        P.op("act", lambda e: e.activation(out=tm[5][:], in_=tm[1][:], func=AF.Ln), reads=[tm_b[1]], writes=[tm_b[5]])
                        P.op("dve", lambda e: e.tensor_tensor_scan(out=tm[2][:], data0=rmask[:], data1=tm[5][:], initial=0.0,
                                                                   op0=ALU.mult, op1=ALU.add),
                             reads=[tm_b[5], const_b], writes=[tm_b[2]])
                        P.op("act", [lambda e: e.activation(out=tm[3][:], in_=tm[2][:], func=AF.Exp),
                                     lambda e: e.activation(out=tm[4][:], in_=tm[2][:], func=AF.Exp, scale=-1.0)],
                             reads=[tm_b[2]], writes=[tm_b[3], tm_b[4]])
                        fk = [(lambda e, h=h, hl=hl, t0=t0: e.scalar_tensor_tensor(out=kt[:, hl, :], in0=t0[:], scalar=lbv[:, 1, h:h + 1],
                                                                            in1=tm[4][:], op0=ALU.mult, op1=ALU.mult)),
                              (lambda e, hl=hl: e.scalar_tensor_tensor(out=qt[:, hl, :], in0=qt[:, hl, :], scalar=float(128 ** -0.5),
                                                                       in1=tm[3][:], op0=ALU.mult, op1=ALU.mult)),
                              (lambda e, hl=hl: e.tensor_copy(out=ebl[:, hl, 0:8],
                                                              in_=tm[3][:, 0:TP].rearrange("p (c k) -> p c k", k=64)[:, :, 63])),
                              (lambda e, hl=hl: e.tensor_copy(out=ebl[:, hl, 8:9], in_=tm[3][:, T - 1:T]))]
                        P.op("dve", fk, reads=[t0_b, tm_b[3], tm_b[4], const_b], writes=[kt_b[hl], qt_b[hl], ebl_b[hl]])
                    wb, wvv = load_w([(w_in[:, 2 * D + g * 512:2 * D + (g + 1) * 512], 0, 512)], NCH, 512)
                    for ci, (c0, cn) in enumerate(chunks):
                        bk = next_bank()
                        P.op("pe", [(lambda e, k=k, bk=bk, c0=c0, cn=cn, wvv=wvv: e.matmul(ps[0:cn, bk, :], hT[:, k, c0:c0 + cn], wvv[:, k, :],
                                                                                  start=(k == 0), stop=(k == NCH - 1)))
                                    for k in range(NCH)], reads=wb + hT_b, writes=bank_all(bk))
                        P.op("act", (lambda e, bk=bk, ci=ci, cn=cn: e.activation(out=vtok[0:cn, ci, :], in_=ps[0:cn, bk, :], func=AF.Copy)),
                             reads=bank_all(bk), writes=[vtok_b[ci]])
                    extra = gz_thunks(g)
                    if g + 1 < 4:
                        extra = extra + q_thunks(g + 1, qt_n, qt_nb)
                    steps = [(ci, hl) for ci in range(9) for hl in range(4)]
                    ctx = {}

                    def stepA(idx):
                        ci, hl = steps[idx]
                        c0, cn = chunks[ci]
                        r = idx % 3
                        bk = next_bank()
                        bka = next_bank()
                        ctx[idx] = (bk, bka, r)
                        P.op("dve", (lambda e, r=r, hl=hl, c0=c0, cn=cn, ci=ci:
                                     e.tensor_scalar(out=khat[r][:, 0:cn], in0=kt[:, hl, c0:c0 + cn], scalar1=ebl[:, hl, ci:ci + 1],
                                                     scalar2=None, op0=ALU.mult)),
                             reads=[kt_b[hl], ebl_b[hl]], writes=[khat_b[r]])
                        P.op("pe", (lambda e, bk=bk, hl=hl, c0=c0, cn=cn:
                                    e.matmul(ps[0:cn, bk, 0:cn], kt[:, hl, c0:c0 + cn], qt[:, hl, c0:c0 + cn], start=True, stop=True)),
                             reads=[kt_b[hl], qt_b[hl]], writes=[bank_b[bk]])
                        P.op("pe", (lambda e, bka=bka, r=r, cn=cn:
                                    e.matmul(ps[0:cn, bka, 64:192], khat[r][:, 0:cn], ident_bf[:], start=True, stop=True)),
                             reads=[khat_b[r], const_b], writes=[bank_b[bka]])
                        P.op("dve", (lambda e, bk=bk, r=r, cn=cn:
                                     e.tensor_tensor(out=asb[r][0:cn, 0:cn], in0=ps[0:cn, bk, 0:cn], in1=tmask[0:cn, 0:cn], op=ALU.mult)),
                             reads=[bank_b[bk], par_b], writes=[asb_b[r]])
                        P.op("act", (lambda e, bka=bka, r=r, cn=cn:
                                     e.activation(out=ktk[r][0:cn, :], in_=ps[0:cn, bka, 64:192], func=AF.Copy)),
                             reads=[bank_b[bka]], writes=[ktk_b[r]])

                    def stepB(idx):
                        ci, hl = steps[idx]
                        c0, cn = chunks[ci]
                        bk, bka, r = ctx[idx]
                        h = 4 * g + hl
                        if ci < 8:
                            S, Sbf, S_b, Sbf_b = Sp, Sp_bf, Sp_b[h], Spbf_b[h]
                        else:
                            S, Sbf, S_b, Sbf_b = Ss, Ss_bf, Ss_b[h], Ssbf_b[h]
                        fo = [lambda e: e.matmul(ps[:, bka, 192:192 + cn], vtok[0:cn, ci, hl * 128:(hl + 1) * 128], asb[r][0:cn, 0:cn],
                                                 start=True, stop=False),
                              lambda e: e.matmul(ps[:, bka, 192:192 + cn], Sbf[:, h, :], qt[:, hl, c0:c0 + cn], start=False, stop=True)]
                        P.op("pe", fo, reads=[vtok_b[ci], asb_b[r], Sbf_b, qt_b[hl]], writes=[bank_b[bka]])
                        P.op("pe", lambda e: e.matmul(ps[:, bk, 256:384], ktk[r][0:cn, :], vtok[0:cn, ci, hl * 128:(hl + 1) * 128],
                                                      start=True, stop=True),
                             reads=[vtok_b[ci], ktk_b[r]], writes=[bank_b[bk]])
                        P.op("act", (lambda e: e.activation(out=oT[:, hl, c0:c0 + cn], in_=ps[:, bka, 192:192 + cn], func=AF.Copy)),
                             reads=[bank_b[bka]], guard=[oT_b[hl]])
                        P.op("dve", (lambda e: e.scalar_tensor_tensor(out=S[:, h, :], in0=S[:, h, :], scalar=ebl[:, hl, ci:ci + 1],
                                                                      in1=ps[:, bk, 256:384], op0=ALU.mult, op1=ALU.add)),
                             reads=[bank_b[bk], ebl_b[hl]], writes=[S_b])
                        P.op("act", (lambda e: e.activation(out=Sbf[:, h, :], in_=S[:, h, :], func=AF.Copy)),
                             reads=[S_b], writes=[Sbf_b])

                    n = len(steps)
                    every = max(1, (n - 2) // len(extra))
                    stepA(0)
                    for idx in range(n):
                        if idx + 1 < n:
                            stepA(idx + 1)
                        stepB(idx)
                        if extra and idx % every == every - 1:
                            extra.pop(0)()
                    while extra:
                        extra.pop(0)()
                    for hl in range(4):
                        h = 4 * g + hl
                        pr = next_pair()
                        oT_b[hl].w = ("act", P.q["act"].count)
                        oT_b[hl].r = {}
                        P.op("act", (lambda e, hl=hl: e.activation(out=tm[0][:], in_=oT[:, hl, :], func=AF.Square)),
                             reads=[oT_b[hl]], writes=[tm_b[0]])
                        P.op("pe", [(lambda e, h_=h_, pr=pr: e.matmul(ps[:, 2 * pr + h_, 0:HALF], ones_f[:], tm[0][:, h_ * HALF:(h_ + 1) * HALF],
                                                                      start=True, stop=True)) for h_ in range(2)],
                             reads=[tm_b[0], const_b], writes=pair_bufs(pr))
                        finish_rstd(pr, 1.0 / 128, tm[1], tm_b[1])
                        P.op("dve", (lambda e, hl=hl: e.scalar_tensor_tensor(out=tm[6][:], in0=oT[:, hl, :], scalar=pcol(R_GN), in1=tm[1][:],
                                                                             op0=ALU.mult, op1=ALU.mult)),
                             reads=[oT_b[hl], tm_b[1], par_b], writes=[tm_b[6]])
                        P.op("dve", (lambda e, hl=hl, h=h: e.tensor_tensor(out=onT[:, h, :], in0=tm[6][:], in1=sgz[:, hl, :], op=ALU.mult)),
                             reads=[tm_b[6], sgz_b[hl]], writes=[onT_b[h]])
                for th in q_thunks(0, qts[0], qt_bs[0]):
                    th()
                for g in range(4):
                    do_group(g, qts[g % 2], qt_bs[g % 2], qts[(g + 1) % 2], qt_bs[(g + 1) % 2])
                P.dma("sp", lambda e: e.dma_start(out=o_hgrn_s[t].rearrange("h k v -> k h v"), in_=Ss[:]), "sout", reads=Ss_b)
                if last:
                    P.dma("sp", lambda e: e.dma_start(out=o_hgrn_p.rearrange("h k v -> k h v"), in_=Sp[:]), "sout", reads=Sp_b)
            with P.scope() as sc:
                hout = sc.sb("houthg", [128, NCH, T], F32)
                hout_b = P.bufs(NCH)
                nsum = out_proj(hgrn_w_out[0], NCH, onT, onT_b, hout, hout_b, TMP_POST, 512)
                postnorm_add(l, 3, hout, hout_b, nsum, 1.0, TMP_POST)

    init_consts()
    stage_list = [("ffn", 0, 0), ("ab",), ("ffn", 0, 1), ("ffn", 1, 0), ("hg",), ("ffn", 1, 1)]
    for t in range(ntiles):
        load_tile(t)
        for si, st in enumerate(stage_list):
            if si >= stop_after:
                break
            if st[0] == "ffn":
                ffn(st[1], st[2], "f")
            elif st[0] == "ab":
                mixer_ab(t)
            else:
                mixer_hgrn(t)
        store_tile(t)

    es = ExitStack()
    sems = {}
    for n in ("pe", "act", "dve", "pool", "sp"):
        sems[n] = es.enter_context(nc.semaphore("s_" + n))
    for k in P.dma_cnt:
        sems[k] = es.enter_context(nc.semaphore("s_" + k.replace(":", "_")))
    final_waits = [(k, v) for k, v in P.dma_cnt.items()] + [(n, P.q[n].count) for n in ("pe", "act", "dve", "pool")]
    block = es.enter_context(nc.Block())

    def run_q(e, qn, final=False):
        q = P.q[qn]
        for waits, fns, kind in q.ops:
            for k, v in waits:
                e.wait_ge(sems[k], v)
            ins = None
            for f in fns:
                ins = f(e)
            if kind is None:
                ins.then_inc(sems[qn], 1)
            else:
                ins.then_inc(sems[kind], 16)
        if final:
            for k, v in final_waits:
                if v > 0:
                    e.wait_ge(sems[k], v)

    @block.tensor
    def _(e):
        run_q(e, "pe")

    @block.scalar
    def _(e):
        run_q(e, "act")

    @block.vector
    def _(e):
        run_q(e, "dve")

    @block.gpsimd
    def _(e):
        run_q(e, "pool")

    @block.sync
    def _(e):
        run_q(e, "sp", final=True)

    es.close()
    top.close()
    return nc, P


_W_NAMES = ["ab_w_in", "ab_w_out", "conv_w", "conv_b", "conv_ln_g", "conv_ln_b", "pool_w", "pool_scale",
            "hgrn_w_in", "hgrn_w_out", "hgrn_gnorm", "hgrn_lb", "ffn_w_in", "ffn_w_out", "norm_g"]


def make_in_maps(inputs, cores):
    f = lambda a: np.ascontiguousarray(np.asarray(a, dtype=np.float32))
    w = {n: f(inputs[n]) for n in _W_NAMES}
    maps = []
    for i in cores:
        m = dict(w)
        m["xp"] = f(inputs["x_prompt"][i])
        m["xs"] = f(inputs["x_sample"][4 * i:4 * i + 4])
        m["cconv"] = f(inputs["cache_conv"][0, 4 * i:4 * i + 4])
        m["cpool"] = f(inputs["cache_pool"][0, 4 * i:4 * i + 4])
        m["shg"] = f(inputs["state_hgrn"][0, 4 * i:4 * i + 4])
        maps.append(m)
    return maps


def kernel(**inputs):
    nc, _ = build()
    maps = make_in_maps(inputs, list(range(8)))
    res = run_bass_kernel_spmd(nc, maps, core_ids=list(range(8)))
    r = res.results
    y_p = np.stack([r[i]["yp"] for i in range(8)], 0).astype(np.float32)
    y_s = np.concatenate([r[i]["ys"] for i in range(8)], 0).astype(np.float32)
    conv_p = np.stack([r[i]["o_conv_p"] for i in range(8)], 0)[None].astype(np.float32)
    pool_p = np.stack([r[i]["o_pool_p"] for i in range(8)], 0)[None].astype(np.float32)
    hgrn_p = np.stack([r[i]["o_hgrn_p"] for i in range(8)], 0)[None].astype(np.float32)
    conv_s = np.concatenate([r[i]["o_conv_s"] for i in range(8)], 0)[None].astype(np.float32)
    pool_s = np.concatenate([r[i]["o_pool_s"] for i in range(8)], 0)[None].astype(np.float32)
    hgrn_s = np.concatenate([r[i]["o_hgrn_s"] for i in range(8)], 0)[None].astype(np.float32)
    return (y_p, y_s, conv_p, pool_p, hgrn_p, conv_s, pool_s, hgrn_s)
```

```python
import numpy as np
from contextlib import ExitStack, contextmanager
import concourse.bass as bass
import concourse.mybir as mybir
from concourse.bass_utils import run_bass_kernel_spmd

F32 = mybir.dt.float32
BF16 = mybir.dt.bfloat16
ALU = mybir.AluOpType
AF = mybir.ActivationFunctionType

D = 2048
NCH = 16
DFF = 5632
NFF = 44
TP = 512
TS = 16
T = TP + TS
HALF = T // 2
NTILE = 4
EPS = 1e-6
R_NG, R_CW, R_CB, R_LG, R_LB, R_PS, R_GN, R_HL, R_END = 0, 192, 440, 448, 456, 464, 472, 473, 505
SLOT_ELEMS = 16 * 512


class Buf:
    def __init__(self, rel=None):
        self.w = None
        self.r = dict(rel) if rel else {}


class Q:
    def __init__(self, name):
        self.name = name
        self.ops = []
        self.count = 0
        self.seen = {}


class Prog:
    def __init__(self, nc):
        self.nc = nc
        self.q = {n: Q(n) for n in ("pe", "act", "dve", "pool", "sp")}
        self.dma_cnt = {}
        self.release = {}
        self.n_inst = 0

    def buf(self):
        return Buf(self.release)

    def bufs(self, n):
        return [Buf(self.release) for _ in range(n)]

    def deps_of(self, bufs):
        out = {}
        for b in bufs:
            if b.w and out.get(b.w[0], 0) < b.w[1]:
                out[b.w[0]] = b.w[1]
            for k, v in b.r.items():
                if out.get(k, 0) < v:
                    out[k] = v
        return list(out.items())

    def _waits(self, q, reads, writes, extra):
        deps = {}

        def add(k, v):
            if deps.get(k, 0) < v:
                deps[k] = v
        for b in reads:
            if b.w:
                add(*b.w)
        for b in writes:
            if b.w:
                add(*b.w)
            for k, v in b.r.items():
                add(k, v)
        for t in extra:
            if t:
                add(*t)
        waits = []
        for k, v in deps.items():
            if q.seen.get(k, 0) < v:
                q.seen[k] = v
                waits.append((k, v))
        return waits

    def op(self, qn, fns, reads=(), writes=(), extra=(), guard=()):
        q = self.q[qn]
        if callable(fns):
            fns = [fns]
        waits = self._waits(q, reads, list(writes) + list(guard), extra)
        q.count += 1
        tok = (qn, q.count)
        q.ops.append((waits, fns, None))
        self.n_inst += len(fns)
        for b in list(reads) + list(guard):
            b.r[qn] = q.count
        for b in writes:
            b.w = tok
            b.r = {}
        return tok

    def dma(self, qn, fn, key, reads=(), writes=(), extra=()):
        q = self.q[qn]
        waits = self._waits(q, reads, writes, extra)
        k = "d:" + key
        self.dma_cnt[k] = self.dma_cnt.get(k, 0) + 16
        tok = (k, self.dma_cnt[k])
        q.ops.append((waits, [fn], k))
        self.n_inst += 1
        for b in reads:
            b.r[k] = tok[1]
        for b in writes:
            b.w = tok
            b.r = {}
        return tok

    def clock(self):
        c = {n: q.count for n, q in self.q.items() if n in ("pe", "act", "dve")}
        for k, v in self.dma_cnt.items():
            if not k.startswith("d:w"):
                c[k] = v
        return c

    @contextmanager
    def scope(self):
        es = ExitStack()
        sc = Scope(self, es)
        try:
            yield sc
        finally:
            es.close()
            self.release = self.clock()


class Scope:
    def __init__(self, P, es):
        self.P = P
        self.es = es
        self.n = 0

    def sb(self, name, shape, dt):
        self.P.uid = getattr(self.P, "uid", 0) + 1
        return self.es.enter_context(self.P.nc.sbuf_tensor("%s_u%d" % (name, self.P.uid), shape, dt))


DBG = {'ffn_phase': 9, 'nsum': 1, 'sq': 1, 'copy': 1, 'kmax': 99}


def build(ntiles=NTILE, stop_after=99):
    nc = bass.Bass("TRN2", target_bir_lowering=False)
    P = Prog(nc)

    def din(name, shape):
        return nc.dram_tensor(name, shape, F32, kind="ExternalInput").ap()

    def dout(name, shape):
        return nc.dram_tensor(name, shape, F32, kind="ExternalOutput").ap()

    xp = din("xp", [2048, D])
    xs = din("xs", [4, TS, D])
    cconv = din("cconv", [4, 30, 1024])
    cpool = din("cpool", [4, 15, 1024])
    shg = din("shg", [4, 16, 128, 128])
    ab_w_in = din("ab_w_in", [1, D, 3072])
    ab_w_out = din("ab_w_out", [1, D, D])
    conv_w = din("conv_w", [1, 31, 1024])
    conv_b = din("conv_b", [1, 1024])
    conv_ln_g = din("conv_ln_g", [1, 1024])
    conv_ln_b = din("conv_ln_b", [1, 1024])
    pool_w = din("pool_w", [1, 4, 256, 256])
    pool_scale = din("pool_scale", [1, 1024])
    hgrn_w_in = din("hgrn_w_in", [1, D, 4 * D])
    hgrn_w_out = din("hgrn_w_out", [1, D, D])
    hgrn_gnorm = din("hgrn_gnorm", [1, 128])
    hgrn_lb = din("hgrn_lb", [2, D])
    ffn_w_in = din("ffn_w_in", [2, 2, D, 2 * DFF])
    ffn_w_out = din("ffn_w_out", [2, 2, DFF, D])
    norm_g = din("norm_g", [2, 6, D])

    yp = dout("yp", [2048, D])
    ys = dout("ys", [4, TS, D])
    o_conv_p = dout("o_conv_p", [30, 1024])
    o_pool_p = dout("o_pool_p", [15, 1024])
    o_hgrn_p = dout("o_hgrn_p", [16, 128, 128])
    o_conv_s = dout("o_conv_s", [4, 30, 1024])
    o_pool_s = dout("o_pool_s", [4, 15, 1024])
    o_hgrn_s = dout("o_hgrn_s", [4, 16, 128, 128])

    top = ExitStack()

    def sbp(name, shape, dt):
        return top.enter_context(nc.sbuf_tensor(name, shape, dt))

    xT = sbp("xT", [128, NCH, T], F32)
    xT_b = P.bufs(NCH)
    Sp = sbp("Sp", [128, 16, 128], F32)
    Sp_bf = sbp("Sp_bf", [128, 16, 128], BF16)
    Sp_b = P.bufs(16)
    Spbf_b = P.bufs(16)
    NSLOT = 3
    wslot = [sbp("wslot%d" % i, [128, SLOT_ELEMS], BF16) for i in range(NSLOT)]
    wslot_b = [P.bufs(4) for _ in range(NSLOT)]
    paramT = sbp("paramT", [128, 512], F32)
    par_b = P.buf()
    ones_bf = sbp("ones_bf", [128, 128], BF16)
    ones_f = sbp("ones_f", [128, 128], F32)
    ident_f = sbp("ident_f", [128, 128], F32)
    ident_bf = sbp("ident_bf", [128, 128], BF16)
    tmask = sbp("tmask", [64, 64], F32)
    rmask = sbp("rmask", [128, T], F32)
    icnt = sbp("icnt", [128, 16], F32)
    lbv = sbp("lbv", [128, 3, 16], F32)
    chist = sbp("chist", [128, 8, 30], F32)
    phist = sbp("phist", [128, 8, 15], F32)
    chist_b = P.buf()
    phist_b = P.buf()
    const_b = P.buf()
    ps = top.enter_context(nc.psum_tensor("ps", [128, 8, 512], F32))
    bank_b = P.bufs(8)
    reg_b = [P.bufs(4) for _ in range(8)]
    state = {"pair": 0, "bank": 0, "slot": 0}

    def next_pair():
        p = state["pair"]
        state["pair"] = (p + 1) % 4
        return p

    def next_bank():
        b = state["bank"]
        state["bank"] = (b + 1) % 8
        return b

    def bank_all(b):
        return [bank_b[b]]

    def pair_bufs(p):
        return bank_all(2 * p) + bank_all(2 * p + 1)

    def pview(p):
        return ps[:, 2 * p:2 * p + 2, 0:HALF]

    def sview(ap2d):
        return ap2d.rearrange("p (h c) -> p h c", h=2)

    def chain(qn, fns, reads, writes):
        tok = None
        for f in fns:
            tok = P.op(qn, f, reads=reads, writes=writes)
        return tok

    def init_consts():
        ms = [lambda e: e.memset(ones_f[:], 1.0), lambda e: e.memset(ones_bf[:], 1.0), lambda e: e.memset(rmask[:], 1.0),
              lambda e: e.memset(Sp[:], 0.0), lambda e: e.memset(Sp_bf[:], 0.0), lambda e: e.memset(chist[:], 0.0),
              lambda e: e.memset(phist[:], 0.0)]
        ms += [(lambda e, t=t: e.memset(icnt[:, t:t + 1], 1.0 / (t + 1))) for t in range(16)]
        P.op("dve", ms, writes=[const_b, chist_b, phist_b] + Sp_b + Spbf_b)
        P.op("dve", [lambda e: e.memset(rmask[:, 0:TP].rearrange("p (c k) -> p c k", k=64)[:, :, 0:1], 0.0),
                     lambda e: e.memset(rmask[:, TP:TP + 1], 0.0)], writes=[const_b])
        idb = P.buf()
        P.op("pool", lambda e: e.affine_select(out=ident_f[:], in_=ones_f[:], pattern=[[-1, 128]], compare_op=ALU.is_equal,
                                               fill=0.0, base=0, channel_multiplier=1), reads=[const_b], writes=[idb])
        P.op("pool", lambda e: e.affine_select(out=tmask[:], in_=ones_f[0:64, 0:64], pattern=[[1, 64]],
                                               compare_op=ALU.is_ge, fill=0.0, base=0, channel_multiplier=-1),
             reads=[const_b], writes=[par_b])
        P.op("act", lambda e: e.activation(out=ident_bf[:], in_=ident_f[:], func=AF.Copy), reads=[idb, par_b], writes=[const_b])
        with P.scope() as sc:
            stg = sc.sb("pstg", [128, 4, 128], F32)
            stg_b = P.buf()
            P.op("dve", lambda e: e.memset(stg[:], 0.0), writes=[stg_b])
            srcs = [(R_NG, norm_g.rearrange("l i (c p) -> (l i c) p", p=128), 192),
                    (R_CW, conv_w[0].rearrange("j (c p) -> (j c) p", p=128), 248),
                    (R_CB, conv_b[0].rearrange("(c p) -> c p", p=128), 8),
                    (R_LG, conv_ln_g[0].rearrange("(c p) -> c p", p=128), 8),
                    (R_LB, conv_ln_b[0].rearrange("(c p) -> c p", p=128), 8),
                    (R_PS, pool_scale[0].rearrange("(c p) -> c p", p=128), 8),
                    (R_GN, hgrn_gnorm, 1),
                    (R_HL, hgrn_lb.rearrange("l (c p) -> (l c) p", p=128), 32)]
            for r0, src, n in srcs:
                done = 0
                while done < n:
                    r = r0 + done
                    blk, off = r // 128, r % 128
                    cnt = min(n - done, 128 - off)
                    P.dma("sp", (lambda e, blk=blk, off=off, cnt=cnt, src=src, done=done:
                                 e.dma_start(out=stg[off:off + cnt, blk, :], in_=src[done:done + cnt, :])),
                          "par", writes=[stg_b])
                    done += cnt
            bk = next_bank()
            P.op("pe", [(lambda e, b=b: e.transpose(ps[:, bk, b * 128:(b + 1) * 128], stg[:, b, :], ident_f[:]))
                        for b in range(4)], reads=[stg_b, const_b, idb], writes=bank_all(bk))
            P.op("act", lambda e: e.activation(out=paramT[:], in_=ps[:, bk, :], func=AF.Copy),
                 reads=bank_all(bk), writes=[par_b])
            P.op("dve", lambda e: e.tensor_tensor(out=lbv[:, 1, :], in0=paramT[:, R_HL + 16:R_HL + 32],
                                                  in1=paramT[:, R_HL:R_HL + 16], op=ALU.subtract),
                 reads=[par_b], writes=[const_b])
            P.op("act", lambda e: e.activation(out=lbv[:, 0, :], in_=lbv[:, 1, :], func=AF.Sigmoid),
                 reads=[const_b], writes=[const_b])
            P.op("dve", lambda e: e.tensor_scalar(out=lbv[:, 1, :], in0=lbv[:, 0, :], scalar1=-1.0, scalar2=1.0,
                                                  op0=ALU.mult, op1=ALU.add), writes=[const_b])
            P.op("dve", lambda e: e.tensor_scalar(out=lbv[:, 2, :], in0=lbv[:, 0, :], scalar1=-1.0, scalar2=None, op0=ALU.add),
                 writes=[const_b, par_b])

    def pcol(r):
        return paramT[:, r:r + 1]

    def load_w(pieces, kc_n, ncols):
        s = state["slot"]
        state["slot"] = (s + 1) % NSLOT
        view = wslot[s][:, 0:kc_n * ncols].rearrange("p (k m) -> p k m", k=kc_n)
        deps = P.deps_of(wslot_b[s])
        for i, pc in enumerate(pieces):
            src, c0, n = pc[0], pc[1], pc[2]
            k0 = pc[3] if len(pc) > 3 else 0
            kn = src.shape[0] // 128
            P.dma("pool", (lambda e, src=src, c0=c0, n=n, view=view, k0=k0, kn=kn:
                           e.dma_start(out=view[:, k0:k0 + kn, c0:c0 + n], in_=src.rearrange("(k p) m -> p k m", p=128))),
                  "w%d" % s, writes=[wslot_b[s][i]], extra=deps)
        for b in wslot_b[s][len(pieces):]:
            b.w = None
            b.r = {}
        return wslot_b[s], view

    def mm_pair(pr, lhs_list, rhs_fn, reads):
        n = len(lhs_list)
        fns = []
        for k in range(n):
            for h in range(2):
                fns.append(lambda e, k=k, h=h: e.matmul(ps[:, 2 * pr + h, 0:HALF], lhs_list[k], rhs_fn(k, h),
                                                         start=(k == 0), stop=(k == n - 1)))
        return P.op("pe", fns, reads=reads, writes=pair_bufs(pr))

    def rhs_of(t3):
        return lambda k, h: t3[:, k, h * HALF:(h + 1) * HALF]

    def finish_rstd(pr, inv_n, rstd, rstd_b, ln_bias=0.0):
        P.op("act", lambda e: e.activation(out=sview(rstd[:]), in_=pview(pr), func=AF.Ln, scale=inv_n, bias=EPS),
             reads=pair_bufs(pr), writes=[rstd_b])
        if ln_bias != 0.0:
            P.op("act", lambda e: e.activation(out=rstd[:], in_=rstd[:], func=AF.Exp, scale=-0.5, bias=ln_bias), writes=[rstd_b])
        else:
            P.op("act", lambda e: e.activation(out=rstd[:], in_=rstd[:], func=AF.Exp, scale=-0.5), writes=[rstd_b])

    def prenorm(l, i, hT, hT_b, tmp):
        sqs, sq_b, rstd, rstd_b = tmp
        pr = next_pair()
        for c in range(NCH):
            ii = c % len(sqs)
            P.op("act", (lambda e, c=c, ii=ii: e.activation(out=sqs[ii][:], in_=xT[:, c, :], func=AF.Square)),
                 reads=[xT_b[c]], writes=[sq_b[ii]])
            P.op("pe", [(lambda e, ii=ii, h=h, c=c: e.matmul(ps[:, 2 * pr + h, 0:HALF], ones_bf[:],
                                                             sqs[ii][:, h * HALF:(h + 1) * HALF],
                                                             start=(c == 0), stop=(c == NCH - 1))) for h in range(2)],
                 reads=[sq_b[ii], const_b], writes=pair_bufs(pr))
        finish_rstd(pr, 1.0 / D, rstd, rstd_b)
        for c in range(NCH):
            P.op("dve", (lambda e, c=c: e.scalar_tensor_tensor(out=hT[:, c, :], in0=xT[:, c, :],
                                                               scalar=pcol(R_NG + (l * 6 + i) * 16 + c), in1=rstd[:],
                                                               op0=ALU.mult, op1=ALU.mult)),
                 reads=[xT_b[c], rstd_b, par_b], writes=[hT_b[c]])

    def postnorm_add(l, i, hout, hout_b, nsum_pr, alpha, tmp):
        sqs, sq_b, rstd, rstd_b = tmp
        finish_rstd(nsum_pr, 1.0 / D, rstd, rstd_b, ln_bias=float(np.log(alpha)) if alpha != 1.0 else 0.0)
        for c in range(NCH):
            P.op("dve", (lambda e, c=c: e.scalar_tensor_tensor(out=hout[:, c, :], in0=hout[:, c, :],
                                                               scalar=pcol(R_NG + (l * 6 + i) * 16 + c),
                                                               in1=rstd[:], op0=ALU.mult, op1=ALU.mult)),
                 reads=[rstd_b, par_b], writes=[hout_b[c]])
        for c in range(NCH):
            P.op("dve", (lambda e, c=c: e.tensor_tensor(out=xT[:, c, :], in0=xT[:, c, :], in1=hout[:, c, :], op=ALU.add)),
                 reads=[hout_b[c]], writes=[xT_b[c]])

    def out_proj(w2d, kc_n, rhs3, rhs_b, hout, hout_b, tmp, gcols):
        sqs, sq_b, rstd, rstd_b = tmp
        nsum = next_pair()
        mi = 0
        pending = None
        for g0 in range(0, D, gcols):
            if kc_n > 16:
                kq = kc_n // 4
                pcs = [(w2d[q * kq * 128:(q + 1) * kq * 128, g0:g0 + gcols], 0, gcols, q * kq) for q in range(4)]
            else:
                pcs = [(w2d[:, g0:g0 + gcols], 0, gcols)]
            wb, wv = load_w(pcs, kc_n, gcols)
            for mc in range(gcols // 128):
                pr = next_pair()
                if pr == nsum:
                    pr = next_pair()
                mm_pair(pr, [wv[:, k, mc * 128:(mc + 1) * 128] for k in range(min(kc_n, DBG['kmax']))], rhs_of(rhs3), wb + rhs_b)
                m = mi
                ii = m % len(sqs)
                if DBG['copy']:
                    P.op("dve", (lambda e, m=m, pr=pr: e.tensor_copy(out=sview(hout[:, m, :]), in_=pview(pr))),
                         reads=pair_bufs(pr), writes=[hout_b[m]])
                if DBG['sq']:
                    P.op("act", (lambda e, ii=ii, m=m: e.activation(out=sqs[ii][:], in_=hout[:, m, :], func=AF.Square)),
                         reads=[hout_b[m]], writes=[sq_b[ii]])
                if pending is not None:
                    pending()

                def pending(ii=ii, m=m):
                    P.op("pe", [(lambda e, ii=ii, h=h, m=m: e.matmul(ps[:, 2 * nsum + h, 0:HALF], ones_bf[:],
                                                                     sqs[ii][:, h * HALF:(h + 1) * HALF],
                                                                     start=(m == 0), stop=(m == NCH - 1))) for h in range(2)],
                         reads=[sq_b[ii], const_b], writes=pair_bufs(nsum))
                mi += 1
        pending()
        return nsum

    def mk_tmp_p(tag):
        sqs = [sbp("sq%s%d" % (tag, i), [128, T], BF16) for i in range(2)]
        sq_b = P.bufs(2)
        rstd = sbp("rstd" + tag, [128, T], F32)
        return (sqs, sq_b, rstd, P.buf())
    TMP_PRE = mk_tmp_p("pre")
    TMP_POST = mk_tmp_p("post")

    def ffn(l, j, tag):
        w_in = ffn_w_in[l, j]
        w_out = ffn_w_out[l, j]
        with P.scope() as sc:
            hT = sc.sb("hT" + tag, [128, NCH, T], BF16)
            hT_b = P.bufs(NCH)
            gT = sc.sb("gT" + tag, [128, NFF, T], BF16)
            gT_b = P.bufs(NFF)
            hout = sc.sb("hout" + tag, [128, NCH, T], F32)
            hout_b = P.bufs(NCH)
            sa = [sc.sb("sa%s%d" % (tag, i), [128, T], F32) for i in range(2)]
            sa_b = P.bufs(2)
            prenorm(l, 0 if j == 0 else 4, hT, hT_b, TMP_PRE)
            for grp in range(NFF // 2 if DBG['ffn_phase'] >= 1 else 0):
                c0 = grp * 256
                wb, wv = load_w([(w_in[:, c0:c0 + 256], 0, 256), (w_in[:, DFF + c0:DFF + c0 + 256], 256, 256)], NCH, 512)
                for jj in range(2):
                    fch = grp * 2 + jj
                    pa = next_pair()
                    pb = next_pair()
                    mm_pair(pa, [wv[:, k, jj * 128:(jj + 1) * 128] for k in range(NCH)], rhs_of(hT), wb + hT_b)
                    mm_pair(pb, [wv[:, k, 256 + jj * 128:256 + (jj + 1) * 128] for k in range(NCH)], rhs_of(hT), wb + hT_b)
                    ii = fch % 2
                    P.op("act", (lambda e, ii=ii, pa=pa: e.activation(out=sview(sa[ii][:]), in_=pview(pa), func=AF.Silu)),
                         reads=pair_bufs(pa), writes=[sa_b[ii]])
                    P.op("dve", (lambda e, ii=ii, pb=pb, fch=fch: e.tensor_tensor(out=sview(gT[:, fch, :]), in0=sview(sa[ii][:]),
                                                                                  in1=pview(pb), op=ALU.mult)),
                         reads=[sa_b[ii]] + pair_bufs(pb), writes=[gT_b[fch]])
            if DBG['ffn_phase'] >= 2:
                nsum = out_proj(w_out, NFF, gT, gT_b, hout, hout_b, TMP_POST, 128)
            if DBG['ffn_phase'] >= 3:
                postnorm_add(l, 1 if j == 0 else 5, hout, hout_b, nsum, 0.5, TMP_POST)

    def load_tile(t):
        with P.scope() as sc:
            stg = [sc.sb("xstg%d" % i, [128, D], F32) for i in range(2)]
            stg_b = P.bufs(2)
            for blk in range(5):
                i = blk % 2
                ntok = 128 if blk < 4 else TS
                src = xp[t * TP + blk * 128:t * TP + (blk + 1) * 128, :] if blk < 4 else xs[t]
                P.dma("sp", (lambda e, i=i, ntok=ntok, src=src: e.dma_start(out=stg[i][0:ntok, :], in_=src)),
                      "xin%d" % i, writes=[stg_b[i]])
                for cg in range(4):
                    bk = next_bank()
                    P.op("pe", [(lambda e, c=c, i=i, ntok=ntok, bk=bk, cg=cg:
                                 e.transpose(ps[:, bk, (c - cg * 4) * 128:(c - cg * 4) * 128 + ntok],
                                             stg[i][0:ntok, c * 128:(c + 1) * 128], ident_f[0:ntok, 0:ntok]))
                                for c in range(cg * 4, cg * 4 + 4)],
                         reads=[stg_b[i], const_b], writes=bank_all(bk))
                    P.op("act", (lambda e, bk=bk, cg=cg, blk=blk, ntok=ntok:
                                 e.activation(out=xT[:, cg * 4:cg * 4 + 4, blk * 128:blk * 128 + ntok],
                                              in_=ps[:, bk, :].rearrange("p (c k) -> p c k", k=128)[:, :, 0:ntok],
                                              func=AF.Copy)),
                         reads=bank_all(bk), guard=xT_b[cg * 4:cg * 4 + 4])
            for c in range(NCH):
                xT_b[c].w = ("act", P.q["act"].count)
                xT_b[c].r = {}

    def store_tile(t):
        with P.scope() as sc:
            stg = [sc.sb("ystg%d" % i, [128, D], F32) for i in range(2)]
            stg_b = [P.bufs(4) for _ in range(2)]
            for blk in range(5):
                i = blk % 2
                ntok = 128 if blk < 4 else TS
                dst = yp[t * TP + blk * 128:t * TP + (blk + 1) * 128, :] if blk < 4 else ys[t]
                for cg in range(4):
                    bk = next_bank()
                    P.op("pe", [(lambda e, c=c, ntok=ntok, bk=bk, cg=cg, blk=blk:
                                 e.transpose(ps[0:ntok, bk, (c - cg * 4) * 128:(c - cg * 4 + 1) * 128],
                                             xT[:, c, blk * 128:blk * 128 + ntok], ident_f[:]))
                                for c in range(cg * 4, cg * 4 + 4)],
                         reads=xT_b[cg * 4:cg * 4 + 4] + [const_b], writes=bank_all(bk))
                    P.op("act", (lambda e, bk=bk, cg=cg, i=i, ntok=ntok:
                                 e.activation(out=stg[i][0:ntok, cg * 512:(cg + 1) * 512], in_=ps[0:ntok, bk, :], func=AF.Copy)),
                         reads=bank_all(bk), writes=[stg_b[i][cg]])
                P.dma("sp", (lambda e, i=i, ntok=ntok, dst=dst: e.dma_start(out=dst, in_=stg[i][0:ntok, :])),
                      "yout%d" % i, reads=stg_b[i])

    def mixer_ab(t):
        l = 0
        last = (t == ntiles - 1)
        AW = 30 + TP + 30 + TS
        PW = 15 + TP + 15 + TS
        CW = AW - 30
        CH = CW // 2
        with P.scope() as sc0:
            m_in = sc0.sb("m_in", [128, NCH, T], BF16)
            m_in_b = P.bufs(NCH)
            with P.scope() as scA:
                abuf = scA.sb("abuf", [128, 8, AW], F32)
                abuf_b = P.bufs(8)
                pbuf = scA.sb("pbuf", [128, 8, PW], F32)
                pbuf_b = P.bufs(8)
                with P.scope() as sc:
                    stg4 = sc.sb("stg4", [32, 2, 1024], F32)
                    stg4_b = [P.bufs(2) for _ in range(2)]
                    hT = sc.sb("hTab", [128, NCH, T], BF16)
                    hT_b = P.bufs(NCH)
                    sg = [sc.sb("sgab%d" % i, [128, T], F32) for i in range(2)]
                    sg_b = P.bufs(2)
                    prenorm(l, 2, hT, hT_b, TMP_PRE)
                    P.op("dve", lambda e: e.tensor_copy(out=abuf[:, :, 0:30], in_=chist[:]), reads=[chist_b], guard=abuf_b)
                    P.op("dve", lambda e: e.tensor_copy(out=pbuf[:, :, 0:15], in_=phist[:]), reads=[phist_b], guard=pbuf_b)
                    P.dma("sp", lambda e: e.dma_start(out=stg4[0:30, 0, :], in_=cconv[t]), "hin0", writes=stg4_b[0])
                    P.dma("sp", lambda e: e.dma_start(out=stg4[0:15, 1, :], in_=cpool[t]), "hin1", writes=stg4_b[1])
                    for which, n, dstbuf, dst_b, off in ((0, 30, abuf, abuf_b, 30 + TP), (1, 15, pbuf, pbuf_b, 15 + TP)):
                        for cg in range(2):
                            bk = next_bank()
                            P.op("pe", [(lambda e, c=c, bk=bk, cg=cg, which=which, n=n:
                                         e.transpose(ps[:, bk, (c - cg * 4) * 32:(c - cg * 4) * 32 + n],
                                                     stg4[0:n, which, c * 128:(c + 1) * 128], ident_f[0:n, 0:n]))
                                        for c in range(cg * 4, cg * 4 + 4)],
                                 reads=stg4_b[which] + [const_b], writes=bank_all(bk))
                            P.op("act", (lambda e, bk=bk, cg=cg, n=n, dstbuf=dstbuf, off=off:
                                         e.activation(out=dstbuf[:, cg * 4:cg * 4 + 4, off:off + n],
                                                      in_=ps[:, bk, 0:128].rearrange("p (c k) -> p c k", k=32)[:, :, 0:n],
                                                      func=AF.Copy)),
                                 reads=bank_all(bk), guard=dst_b[cg * 4:cg * 4 + 4])
                    w_in = ab_w_in[0]
                    for grp in range(4):
                        c0 = grp * 256
                        wb, wv = load_w([(w_in[:, c0:c0 + 256], 0, 256), (w_in[:, 1024 + c0:1024 + c0 + 256], 256, 256)], NCH, 512)
                        for jj in range(2):
                            ch = grp * 2 + jj
                            pa = next_pair()
                            pb = next_pair()
                            mm_pair(pa, [wv[:, k, jj * 128:(jj + 1) * 128] for k in range(NCH)], rhs_of(hT), wb + hT_b)
                            mm_pair(pb, [wv[:, k, 256 + jj * 128:256 + (jj + 1) * 128] for k in range(NCH)], rhs_of(hT), wb + hT_b)
                            ii = ch % 2
                            P.op("act", (lambda e, ii=ii, pb=pb: e.activation(out=sview(sg[ii][:]), in_=pview(pb), func=AF.Sigmoid)),
                                 reads=pair_bufs(pb), writes=[sg_b[ii]])
                            fa = [
                                (lambda e, ii=ii, pa=pa, ch=ch: e.tensor_tensor(out=abuf[:, ch, 30:30 + HALF], in0=sg[ii][:, 0:HALF],
                                                                                in1=ps[:, 2 * pa, 0:HALF], op=ALU.mult)),
                                (lambda e, ii=ii, pa=pa, ch=ch: e.tensor_tensor(out=abuf[:, ch, 30 + HALF:30 + TP], in0=sg[ii][:, HALF:TP],
                                                                                in1=ps[:, 2 * pa + 1, 0:TP - HALF], op=ALU.mult)),
                                (lambda e, ii=ii, pa=pa, ch=ch: e.tensor_tensor(out=abuf[:, ch, 60 + TP:60 + T], in0=sg[ii][:, TP:T],
                                                                                in1=ps[:, 2 * pa + 1, TP - HALF:HALF], op=ALU.mult))]
                            P.op("dve", fa, reads=[sg_b[ii]] + pair_bufs(pa), guard=[abuf_b[ch]])
                    for grp in range(2):
                        c0 = 2048 + grp * 512
                        wb, wv = load_w([(w_in[:, c0:c0 + 512], 0, 512)], NCH, 512)
                        for jj in range(4):
                            ch = grp * 4 + jj
                            pa = next_pair()
                            mm_pair(pa, [wv[:, k, jj * 128:(jj + 1) * 128] for k in range(NCH)], rhs_of(hT), wb + hT_b)
                            fp = [
                                (lambda e, pa=pa, ch=ch: e.activation(out=pbuf[:, ch, 15:15 + HALF], in_=ps[:, 2 * pa, 0:HALF], func=AF.Copy)),
                                (lambda e, pa=pa, ch=ch: e.activation(out=pbuf[:, ch, 15 + HALF:15 + TP], in_=ps[:, 2 * pa + 1, 0:TP - HALF],
                                                                      func=AF.Copy)),
                                (lambda e, pa=pa, ch=ch: e.activation(out=pbuf[:, ch, 30 + TP:30 + T], in_=ps[:, 2 * pa + 1, TP - HALF:HALF],
                                                                      func=AF.Copy))]
                            P.op("act", fp, reads=pair_bufs(pa), guard=[pbuf_b[ch]])
                    P.op("dve", lambda e: e.tensor_copy(out=chist[:], in_=abuf[:, :, TP:TP + 30]), writes=[chist_b] + abuf_b)
                    P.op("dve", lambda e: e.tensor_copy(out=phist[:], in_=pbuf[:, :, TP:TP + 15]), writes=[phist_b] + pbuf_b)
                with P.scope() as sco:
                    stg4 = sco.sb("stg4o", [32, 2, 1024], F32)
                    stg4_b = [P.bufs(2) for _ in range(2)]
                    outs = [(abuf, abuf_b, 30 + TP + 16, 30, o_conv_s[t], 0), (pbuf, pbuf_b, 15 + TP + 16, 15, o_pool_s[t], 1)]
                    if last:
                        outs += [(abuf, abuf_b, TP, 30, o_conv_p, 0), (pbuf, pbuf_b, TP, 15, o_pool_p, 1)]
                    for srcbuf, src_b, off, n, dst, oi in outs:
                        for cg in range(2):
                            bk = next_bank()
                            P.op("pe", [(lambda e, c=c, bk=bk, cg=cg, srcbuf=srcbuf, off=off, n=n:
                                         e.transpose(ps[0:n, bk, (c - cg * 4) * 128:(c - cg * 4 + 1) * 128],
                                                     srcbuf[:, c, off:off + n], ident_f[:]))
                                        for c in range(cg * 4, cg * 4 + 4)],
                                 reads=src_b[cg * 4:cg * 4 + 4] + [const_b], writes=bank_all(bk))
                            P.op("act", (lambda e, bk=bk, cg=cg, n=n, oi=oi:
                                         e.activation(out=stg4[0:n, oi, cg * 512:(cg + 1) * 512], in_=ps[0:n, bk, :], func=AF.Copy)),
                                 reads=bank_all(bk), writes=[stg4_b[oi][cg]])
                        P.dma("sp", (lambda e, n=n, oi=oi, dst=dst: e.dma_start(out=dst, in_=stg4[0:n, oi, :])),
                              "cout%d" % oi, reads=stg4_b[oi])
                with P.scope() as sc:
                    pt = [sc.sb("ptmp%d" % i, [128, 2, PW], F32) for i in range(2)]
                    pt_b = P.bufs(2)
                    pd = sc.sb("pdiff", [128, 8, T], BF16)
                    pd_b = P.bufs(8)
                    for gi in range(4):
                        win = 2 << gi
                        cs = slice(2 * gi, 2 * gi + 2)
                        bb = [pt[0][:, :, :], pt[1][:, :, :]]
                        gb = pt_b + pd_b[2 * gi:2 * gi + 2]
                        rb = pbuf_b[2 * gi:2 * gi + 2] + [const_b]
                        fl = [lambda e, bb=bb, cs=cs: e.tensor_tensor(out=bb[0][:, :, 1:PW], in0=pbuf[:, cs, 1:PW],
                                                                      in1=pbuf[:, cs, 0:PW - 1], op=ALU.add)]
                        cur, k, lo = 0, 2, 1
                        while k < win:
                            nxt = 1 - cur
                            fl.append(lambda e, bb=bb, cur=cur, nxt=nxt, lo=lo, k=k:
                                      e.tensor_tensor(out=bb[nxt][:, :, lo + k:PW], in0=bb[cur][:, :, lo + k:PW],
                                                      in1=bb[cur][:, :, lo:PW - k], op=ALU.add))
                            lo += k
                            k *= 2
                            cur = nxt
                        res = bb[cur]
                        fl.append(lambda e, res=res, cs=cs, win=win:
                                  e.scalar_tensor_tensor(out=pd[:, cs, 0:TP], in0=res[:, :, 15:15 + TP], scalar=1.0 / win,
                                                         in1=pbuf[:, cs, 15:15 + TP], op0=ALU.mult, op1=ALU.subtract))
                        fl.append(lambda e, res=res, cs=cs, win=win:
                                  e.scalar_tensor_tensor(out=pd[:, cs, TP:T], in0=res[:, :, 30 + TP:30 + T], scalar=1.0 / win,
                                                         in1=pbuf[:, cs, 30 + TP:30 + T], op0=ALU.mult, op1=ALU.subtract))
                        if t == 0:
                            for kk in range(2):
                                cc = 2 * gi + kk
                                fl.append(lambda e, res=res, kk=kk, win=win:
                                          e.tensor_tensor(out=res[:, kk, 15:15 + win - 1], in0=res[:, kk, 15:15 + win - 1],
                                                          in1=icnt[:, 0:win - 1], op=ALU.mult))
                                fl.append(lambda e, res=res, kk=kk, cc=cc, win=win:
                                          e.tensor_tensor(out=pd[:, cc, 0:win - 1], in0=res[:, kk, 15:15 + win - 1],
                                                          in1=pbuf[:, cc, 15:15 + win - 1], op=ALU.subtract))
                        chain("dve", fl, rb, gb)
                    wb, wv = load_w([(pool_w[0, g_], g_ * 256, 256) for g_ in range(4)], 2, 1024)
                    for dch in range(8):
                        gi = dch // 2
                        pa = next_pair()
                        mm_pair(pa, [wv[:, k, gi * 256 + (dch % 2) * 128:gi * 256 + (dch % 2 + 1) * 128] for k in range(2)],
                                (lambda k, h, gi=gi: pd[:, 2 * gi + k, h * HALF:(h + 1) * HALF]), wb + pd_b[2 * gi:2 * gi + 2])
                        P.op("act", (lambda e, pa=pa, dch=dch: e.activation(out=sview(m_in[:, 8 + dch, :]), in_=pview(pa), func=AF.Copy,
                                                                            scale=pcol(R_PS + dch))),
                             reads=pair_bufs(pa) + [par_b], writes=[m_in_b[8 + dch]])
                with P.scope() as sc:
                    ybuf = sc.sb("ybuf", [128, 8, CW], F32)
                    ybuf_b = P.bufs(8)
                    for c in range(8):
                        P.op("dve", (lambda e, c=c: e.tensor_scalar(out=ybuf[:, c, :], in0=abuf[:, c, 0:CW], scalar1=pcol(R_CW + c),
                                                                    scalar2=pcol(R_CB + c), op0=ALU.mult, op1=ALU.add)),
                             reads=[abuf_b[c], par_b], writes=[ybuf_b[c]])
                    for j in range(1, 31):
                        for c in range(8):
                            P.op("dve", (lambda e, c=c, j=j: e.scalar_tensor_tensor(out=ybuf[:, c, :], in0=abuf[:, c, j:j + CW],
                                                                                    scalar=pcol(R_CW + j * 8 + c), in1=ybuf[:, c, :],
                                                                                    op0=ALU.mult, op1=ALU.add)),
                                 reads=[abuf_b[c]], writes=[ybuf_b[c]])
                    ysq = sc.sb("ysq", [128, 2, CW], F32)
                    ysq_b = P.bufs(2)
                    p1 = next_pair()
                    p2 = next_pair()
                    for c in range(8):
                        ii = c % 2
                        P.op("dve", (lambda e, c=c, ii=ii: e.tensor_tensor(out=ysq[:, ii, :], in0=ybuf[:, c, :], in1=ybuf[:, c, :], op=ALU.mult)),
                             reads=[ybuf_b[c]], writes=[ysq_b[ii]])
                        P.op("pe", [(lambda e, c=c, h=h: e.matmul(ps[:, 2 * p1 + h, 0:CH], ones_f[:], ybuf[:, c, h * CH:(h + 1) * CH],
                                                                  start=(c == 0), stop=(c == 7))) for h in range(2)],
                             reads=[ybuf_b[c], const_b], writes=pair_bufs(p1))
                        P.op("pe", [(lambda e, c=c, h=h, ii=ii: e.matmul(ps[:, 2 * p2 + h, 0:CH], ones_f[:], ysq[:, ii, h * CH:(h + 1) * CH],
                                                                         start=(c == 0), stop=(c == 7))) for h in range(2)],
                             reads=[ysq_b[ii], const_b], writes=pair_bufs(p2))
                    mean = sc.sb("lnmean", [128, CW], F32)
                    lrstd = sc.sb("lnrstd", [128, CW], F32)
                    mean_b = P.buf()
                    ln_b = P.buf()

                    def cview(ap2d):
                        return ap2d.rearrange("p (h c) -> p h c", h=2)
                    P.op("dve", lambda e: e.tensor_scalar(out=cview(mean[:]), in0=ps[:, 2 * p1:2 * p1 + 2, 0:CH], scalar1=1.0 / 1024,
                                                          scalar2=None, op0=ALU.mult),
                         reads=pair_bufs(p1), writes=[mean_b])
                    P.op("dve", lambda e: e.tensor_tensor(out=lrstd[:], in0=mean[:], in1=mean[:], op=ALU.mult), reads=[mean_b], writes=[ln_b])
                    P.op("dve", lambda e: e.scalar_tensor_tensor(out=cview(lrstd[:]), in0=ps[:, 2 * p2:2 * p2 + 2, 0:CH], scalar=1.0 / 1024,
                                                                 in1=cview(lrstd[:]), op0=ALU.mult, op1=ALU.subtract),
                         reads=pair_bufs(p2), writes=[ln_b])
                    P.op("dve", lambda e: e.tensor_scalar(out=lrstd[:], in0=lrstd[:], scalar1=0.0, scalar2=None, op0=ALU.max), writes=[ln_b])
                    P.op("act", lambda e: e.activation(out=lrstd[:], in_=lrstd[:], func=AF.Ln, bias=EPS), writes=[ln_b])
                    P.op("act", lambda e: e.activation(out=lrstd[:], in_=lrstd[:], func=AF.Exp, scale=-0.5), writes=[ln_b])
                    for c in range(8):
                        P.op("dve", (lambda e, c=c: e.tensor_tensor(out=ybuf[:, c, :], in0=ybuf[:, c, :], in1=mean[:], op=ALU.subtract)),
                             reads=[mean_b], writes=[ybuf_b[c]])
                    for c in range(8):
                        P.op("dve", (lambda e, c=c: e.tensor_tensor(out=ybuf[:, c, :], in0=ybuf[:, c, :], in1=lrstd[:], op=ALU.mult)),
                             reads=[ln_b], writes=[ybuf_b[c]])
                        P.op("act", [(lambda e, c=c: e.activation(out=m_in[:, c, 0:TP], in_=ybuf[:, c, 0:TP], func=AF.Silu,
                                                                  scale=pcol(R_LG + c), bias=pcol(R_LB + c))),
                                     (lambda e, c=c: e.activation(out=m_in[:, c, TP:T], in_=ybuf[:, c, 30 + TP:30 + T], func=AF.Silu,
                                                                  scale=pcol(R_LG + c), bias=pcol(R_LB + c)))],
                             reads=[ybuf_b[c], par_b], writes=[m_in_b[c]])
            with P.scope() as sc:
                hout = sc.sb("houtab", [128, NCH, T], F32)
                hout_b = P.bufs(NCH)
                nsum = out_proj(ab_w_out[0], NCH, m_in, m_in_b, hout, hout_b, TMP_POST, 512)
                postnorm_add(l, 3, hout, hout_b, nsum, 1.0, TMP_POST)

    def mixer_hgrn(t):
        l = 1
        last = (t == ntiles - 1)
        w_in = hgrn_w_in[0]
        chunks = [(ci * 64, 64) for ci in range(8)] + [(TP, TS)]
        with P.scope() as sc0:
            onT = sc0.sb("onT", [128, NCH, T], BF16)
            onT_b = P.bufs(NCH)
            with P.scope() as sc:
                hT = sc.sb("hThg", [128, NCH, T], BF16)
                hT_b = P.bufs(NCH)
                Ss = sc.sb("Ss", [128, 16, 128], F32)
                Ss_bf = sc.sb("Ss_bf", [128, 16, 128], BF16)
                Ss_b = P.bufs(16)
                Ssbf_b = P.bufs(16)
                P.dma("sp", lambda e: e.dma_start(out=Ss[:], in_=shg[t].rearrange("h k v -> k h v")), "sin", writes=Ss_b)
                P.op("act", lambda e: e.activation(out=Ss_bf[:], in_=Ss[:], func=AF.Copy), reads=Ss_b, writes=Ssbf_b)
                qts = [sc.sb("qt%d" % i, [128, 4, T], BF16) for i in range(2)]
                qt_bs = [P.bufs(4) for _ in range(2)]
                kt = sc.sb("kt", [128, 4, T], BF16)
                sgz = sc.sb("sgz", [128, 4, T], BF16)
                ebl = sc.sb("ebl", [128, 4, 16], F32)
                vtok = sc.sb("vtok", [64, 9, 512], BF16)
                oT = sc.sb("oT", [128, 4, T], F32)
                kt_b, sgz_b, ebl_b, oT_b = P.bufs(4), P.bufs(4), P.bufs(4), P.bufs(4)
                vtok_b = P.bufs(9)
                tm = [sc.sb("hgt%d" % i, [128, T], F32) for i in range(7)]
                tm_b = P.bufs(7)
                tm0x = [tm[0], sc.sb("hgt0b", [128, T], F32)]
                tm0x_b = [tm_b[0], P.buf()]
                khat = [sc.sb("khat%d" % i, [128, 64], BF16) for i in range(4)]
                khat_b = P.bufs(4)
                asb = [sc.sb("asb%d" % i, [64, 64], BF16) for i in range(4)]
                asb_b = P.bufs(4)
                ktk = [sc.sb("ktk%d" % i, [64, 128], BF16) for i in range(4)]
                ktk_b = P.bufs(4)
                prenorm(l, 2, hT, hT_b, TMP_PRE)
                def q_thunks(g, qt, qt_b):
                    wb, wq = load_w([(w_in[:, g * 512:(g + 1) * 512], 0, 512)], NCH, 512)

                    def mk(hl):
                        def th():
                            pr = next_pair()
                            mm_pair(pr, [wq[:, k, hl * 128:(hl + 1) * 128] for k in range(NCH)], rhs_of(hT), wb + hT_b)
                            P.op("act", (lambda e, pr=pr, hl=hl: e.activation(out=sview(qt[:, hl, :]), in_=pview(pr), func=AF.Silu)),
                                 reads=pair_bufs(pr), writes=[qt_b[hl]])
                        return th
                    return [mk(hl) for hl in range(4)]

                def gz_thunks(g):
                    wb, wg = load_w([(w_in[:, 3 * D + g * 512:3 * D + (g + 1) * 512], 0, 512)], NCH, 512)

                    def mk(hl):
                        def th():
                            pr = next_pair()
                            mm_pair(pr, [wg[:, k, hl * 128:(hl + 1) * 128] for k in range(NCH)], rhs_of(hT), wb + hT_b)
                            P.op("act", (lambda e, pr=pr, hl=hl: e.activation(out=sview(sgz[:, hl, :]), in_=pview(pr), func=AF.Silu)),
                                 reads=pair_bufs(pr), writes=[sgz_b[hl]])
                        return th
                    return [mk(hl) for hl in range(4)]

                def do_group(g, qt, qt_b, qt_n, qt_nb):
                    wb, wf = load_w([(w_in[:, D + g * 512:D + (g + 1) * 512], 0, 512)], NCH, 512)
                    for hl in range(4):
                        h = 4 * g + hl
                        pr = next_pair()
                        mm_pair(pr, [wf[:, k, hl * 128:(hl + 1) * 128] for k in range(NCH)], rhs_of(hT), wb + hT_b)
                        t0 = tm0x[hl % 2]
                        t0_b = tm0x_b[hl % 2]
                        P.op("act", (lambda e, pr=pr, t0=t0: e.activation(out=sview(t0[:]), in_=pview(pr), func=AF.Sigmoid, scale=-1.0)),
                             reads=pair_bufs(pr), writes=[t0_b])
                        P.op("dve", (lambda e, h=h, t0=t0: e.tensor_scalar(out=tm[1][:], in0=t0[:], scalar1=lbv[:, 2, h:h + 1], scalar2=1.0,
                                                                           op0=ALU.mult, op1=ALU.add)),
                             reads=[t0_b, const_b], writes=[tm_b[1]])
                        P.op("act", lambda e: e.activation(out=tm[5][:], in_=tm[1][:], func=AF.Ln), reads=[tm_b[1]], writes=[tm_b[5]])
                        P.op("dve", lambda e: e.tensor_tensor_scan(out=tm[2][:], data0=rmask[:], data1=tm[5][:], initial=0.0,
                                                                   op0=ALU.mult, op1=ALU.add),
                             reads=[tm_b[5], const_b], writes=[tm_b[2]])
                        P.op("act", [lambda e: e.activation(out=tm[3][:], in_=tm[2][:], func=AF.Exp),
                                     lambda e: e.activation(out=tm[4][:], in_=tm[2][:], func=AF.Exp, scale=-1.0)],
                             reads=[tm_b[2]], writes=[tm_b[3], tm_b[4]])
                        fk = [(lambda e, h=h, hl=hl, t0=t0: e.scalar_tensor_tensor(out=kt[:, hl, :], in0=t0[:], scalar=lbv[:, 1, h:h + 1],
                                                                            in1=tm[4][:], op0=ALU.mult, op1=ALU.mult)),
                              (lambda e, hl=hl: e.scalar_tensor_tensor(out=qt[:, hl, :], in0=qt[:, hl, :], scalar=float(128 ** -0.5),
                                                                       in1=tm[3][:], op0=ALU.mult, op1=ALU.mult)),
                              (lambda e, hl=hl: e.tensor_copy(out=ebl[:, hl, 0:8],
                                                              in_=tm[3][:, 0:TP].rearrange("p (c k) -> p c k", k=64)[:, :, 63])),
                              (lambda e, hl=hl: e.tensor_copy(out=ebl[:, hl, 8:9], in_=tm[3][:, T - 1:T]))]
                        P.op("dve", fk, reads=[t0_b, tm_b[3], tm_b[4], const_b], writes=[kt_b[hl], qt_b[hl], ebl_b[hl]])
                    wb, wvv = load_w([(w_in[:, 2 * D + g * 512:2 * D + (g + 1) * 512], 0, 512)], NCH, 512)
                    for ci, (c0, cn) in enumerate(chunks):
                        bk = next_bank()
                        P.op("pe", [(lambda e, k=k, bk=bk, c0=c0, cn=cn, wvv=wvv: e.matmul(ps[0:cn, bk, :], hT[:, k, c0:c0 + cn], wvv[:, k, :],
                                                                                  start=(k == 0), stop=(k == NCH - 1)))
                                    for k in range(NCH)], reads=wb + hT_b, writes=bank_all(bk))
                        P.op("act", (lambda e, bk=bk, ci=ci, cn=cn: e.activation(out=vtok[0:cn, ci, :], in_=ps[0:cn, bk, :], func=AF.Copy)),
                             reads=bank_all(bk), writes=[vtok_b[ci]])
                    extra = gz_thunks(g)
                    if g + 1 < 4:
                        extra = extra + q_thunks(g + 1, qt_n, qt_nb)
                    steps = [(ci, hl) for ci in range(9) for hl in range(4)]
                    ctx = {}

                    def stepA(idx):
                        ci, hl = steps[idx]
                        c0, cn = chunks[ci]
                        r = idx % 4
                        bk = next_bank()
                        bka = next_bank()
                        ctx[idx] = (bk, bka, r)
                        P.op("dve", (lambda e, r=r, hl=hl, c0=c0, cn=cn, ci=ci:
                                     e.tensor_scalar(out=khat[r][:, 0:cn], in0=kt[:, hl, c0:c0 + cn], scalar1=ebl[:, hl, ci:ci + 1],
                                                     scalar2=None, op0=ALU.mult)),
                             reads=[kt_b[hl], ebl_b[hl]], writes=[khat_b[r]])
                        P.op("pe", (lambda e, bk=bk, hl=hl, c0=c0, cn=cn:
                                    e.matmul(ps[0:cn, bk, 0:cn], kt[:, hl, c0:c0 + cn], qt[:, hl, c0:c0 + cn], start=True, stop=True)),
                             reads=[kt_b[hl], qt_b[hl]], writes=[bank_b[bk]])
                        P.op("pe", (lambda e, bka=bka, r=r, cn=cn:
                                    e.matmul(ps[0:cn, bka, 64:192], khat[r][:, 0:cn], ident_bf[:], start=True, stop=True)),
                             reads=[khat_b[r], const_b], writes=[bank_b[bka]])
                        P.op("dve", (lambda e, bk=bk, r=r, cn=cn:
                                     e.tensor_tensor(out=asb[r][0:cn, 0:cn], in0=ps[0:cn, bk, 0:cn], in1=tmask[0:cn, 0:cn], op=ALU.mult)),
                             reads=[bank_b[bk], par_b], writes=[asb_b[r]])
                        P.op("act", (lambda e, bka=bka, r=r, cn=cn:
                                     e.activation(out=ktk[r][0:cn, :], in_=ps[0:cn, bka, 64:192], func=AF.Copy)),
                             reads=[bank_b[bka]], writes=[ktk_b[r]])

                    def stepB(idx):
                        ci, hl = steps[idx]
                        c0, cn = chunks[ci]
                        bk, bka, r = ctx[idx]
                        h = 4 * g + hl
                        if ci < 8:
                            S, Sbf, S_b, Sbf_b = Sp, Sp_bf, Sp_b[h], Spbf_b[h]
                        else:
                            S, Sbf, S_b, Sbf_b = Ss, Ss_bf, Ss_b[h], Ssbf_b[h]
                        fo = [lambda e: e.matmul(ps[:, bka, 192:192 + cn], vtok[0:cn, ci, hl * 128:(hl + 1) * 128], asb[r][0:cn, 0:cn],
                                                 start=True, stop=False),
                              lambda e: e.matmul(ps[:, bka, 192:192 + cn], Sbf[:, h, :], qt[:, hl, c0:c0 + cn], start=False, stop=True)]
                        P.op("pe", fo, reads=[vtok_b[ci], asb_b[r], Sbf_b, qt_b[hl]], writes=[bank_b[bka]])
                        P.op("pe", lambda e: e.matmul(ps[:, bk, 256:384], ktk[r][0:cn, :], vtok[0:cn, ci, hl * 128:(hl + 1) * 128],
                                                      start=True, stop=True),
                             reads=[vtok_b[ci], ktk_b[r]], writes=[bank_b[bk]])
                        P.op("act", (lambda e: e.activation(out=oT[:, hl, c0:c0 + cn], in_=ps[:, bka, 192:192 + cn], func=AF.Copy)),
                             reads=[bank_b[bka]], guard=[oT_b[hl]])
                        P.op("dve", (lambda e: e.scalar_tensor_tensor(out=S[:, h, :], in0=S[:, h, :], scalar=ebl[:, hl, ci:ci + 1],
                                                                      in1=ps[:, bk, 256:384], op0=ALU.mult, op1=ALU.add)),
                             reads=[bank_b[bk], ebl_b[hl]], writes=[S_b])
                        P.op("act", (lambda e: e.activation(out=Sbf[:, h, :], in_=S[:, h, :], func=AF.Copy)),
                             reads=[S_b], writes=[Sbf_b])

                    n = len(steps)
                    every = max(1, (n - 2) // len(extra))
                    LOOK = 2
                    for i0 in range(LOOK):
                        stepA(i0)
                    for idx in range(n):
                        if idx + LOOK < n:
                            stepA(idx + LOOK)
                        stepB(idx)
                        if extra and idx % every == every - 1:
                            extra.pop(0)()
                    while extra:
                        extra.pop(0)()
                    for hl in range(4):
                        h = 4 * g + hl
                        pr = next_pair()
                        oT_b[hl].w = ("act", P.q["act"].count)
                        oT_b[hl].r = {}
                        P.op("act", (lambda e, hl=hl: e.activation(out=tm[0][:], in_=oT[:, hl, :], func=AF.Square)),
                             reads=[oT_b[hl]], writes=[tm_b[0]])
                        P.op("pe", [(lambda e, h_=h_, pr=pr: e.matmul(ps[:, 2 * pr + h_, 0:HALF], ones_f[:], tm[0][:, h_ * HALF:(h_ + 1) * HALF],
                                                                      start=True, stop=True)) for h_ in range(2)],
                             reads=[tm_b[0], const_b], writes=pair_bufs(pr))
                        finish_rstd(pr, 1.0 / 128, tm[1], tm_b[1])
                        P.op("dve", (lambda e, hl=hl: e.scalar_tensor_tensor(out=tm[6][:], in0=oT[:, hl, :], scalar=pcol(R_GN), in1=tm[1][:],
                                                                             op0=ALU.mult, op1=ALU.mult)),
                             reads=[oT_b[hl], tm_b[1], par_b], writes=[tm_b[6]])
                        P.op("dve", (lambda e, hl=hl, h=h: e.tensor_tensor(out=onT[:, h, :], in0=tm[6][:], in1=sgz[:, hl, :], op=ALU.mult)),
                             reads=[tm_b[6], sgz_b[hl]], writes=[onT_b[h]])
                for th in q_thunks(0, qts[0], qt_bs[0]):
                    th()
                for g in range(4):
                    do_group(g, qts[g % 2], qt_bs[g % 2], qts[(g + 1) % 2], qt_bs[(g + 1) % 2])
                P.dma("sp", lambda e: e.dma_start(out=o_hgrn_s[t].rearrange("h k v -> k h v"), in_=Ss[:]), "sout", reads=Ss_b)
                if last:
                    P.dma("sp", lambda e: e.dma_start(out=o_hgrn_p.rearrange("h k v -> k h v"), in_=Sp[:]), "sout", reads=Sp_b)
            with P.scope() as sc:
                hout = sc.sb("houthg", [128, NCH, T], F32)
                hout_b = P.bufs(NCH)
                nsum = out_proj(hgrn_w_out[0], NCH, onT, onT_b, hout, hout_b, TMP_POST, 512)
                postnorm_add(l, 3, hout, hout_b, nsum, 1.0, TMP_POST)

    init_consts()
    stage_list = [("ffn", 0, 0), ("ab",), ("ffn", 0, 1), ("ffn", 1, 0), ("hg",), ("ffn", 1, 1)]
    for t in range(ntiles):
        load_tile(t)
        for si, st in enumerate(stage_list):
            if si >= stop_after:
                break
            if st[0] == "ffn":
                ffn(st[1], st[2], "f")
            elif st[0] == "ab":
                mixer_ab(t)
            else:
                mixer_hgrn(t)
        store_tile(t)

    es = ExitStack()
    sems = {}
    for n in ("pe", "act", "dve", "pool", "sp"):
        sems[n] = es.enter_context(nc.semaphore("s_" + n))
    for k in P.dma_cnt:
        sems[k] = es.enter_context(nc.semaphore("s_" + k.replace(":", "_")))
    final_waits = [(k, v) for k, v in P.dma_cnt.items()] + [(n, P.q[n].count) for n in ("pe", "act", "dve")]
    block = es.enter_context(nc.Block())

    def run_q(e, qn, final=False):
        q = P.q[qn]
        for waits, fns, kind in q.ops:
            for k, v in waits:
                e.wait_ge(sems[k], v)
            ins = None
            for f in fns:
                ins = f(e)
            if kind is None:
                ins.then_inc(sems[qn], 1)
            else:
                ins.then_inc(sems[kind], 16)
        if final:
            for k, v in final_waits:
                if v > 0:
                    e.wait_ge(sems[k], v)

    @block.tensor
    def _(e):
        run_q(e, "pe")

    @block.scalar
    def _(e):
        run_q(e, "act")

    @block.vector
    def _(e):
        run_q(e, "dve")

    @block.gpsimd
    def _(e):
        run_q(e, "pool")

    @block.sync
    def _(e):
        run_q(e, "sp", final=True)

    es.close()
    top.close()
    return nc, P


_W_NAMES = ["ab_w_in", "ab_w_out", "conv_w", "conv_b", "conv_ln_g", "conv_ln_b", "pool_w", "pool_scale",
            "hgrn_w_in", "hgrn_w_out", "hgrn_gnorm", "hgrn_lb", "ffn_w_in", "ffn_w_out", "norm_g"]


def make_in_maps(inputs, cores):
    f = lambda a: np.ascontiguousarray(np.asarray(a, dtype=np.float32))
    w = {n: f(inputs[n]) for n in _W_NAMES}
    maps = []
    for i in cores:
        m = dict(w)
        m["xp"] = f(inputs["x_prompt"][i])
        m["xs"] = f(inputs["x_sample"][4 * i:4 * i + 4])
        m["cconv"] = f(inputs["cache_conv"][0, 4 * i:4 * i + 4])
        m["cpool"] = f(inputs["cache_pool"][0, 4 * i:4 * i + 4])
        m["shg"] = f(inputs["state_hgrn"][0, 4 * i:4 * i + 4])
        maps.append(m)
    return maps


def kernel(**inputs):
    nc, _ = build()
    maps = make_in_maps(inputs, list(range(8)))
    res = run_bass_kernel_spmd(nc, maps, core_ids=list(range(8)))
    r = res.results
    y_p = np.stack([r[i]["yp"] for i in range(8)], 0).astype(np.float32)
    y_s = np.concatenate([r[i]["ys"] for i in range(8)], 0).astype(np.float32)
    conv_p = np.stack([r[i]["o_conv_p"] for i in range(8)], 0)[None].astype(np.float32)
    pool_p = np.stack([r[i]["o_pool_p"] for i in range(8)], 0)[None].astype(np.float32)
    hgrn_p = np.stack([r[i]["o_hgrn_p"] for i in range(8)], 0)[None].astype(np.float32)
    conv_s = np.concatenate([r[i]["o_conv_s"] for i in range(8)], 0)[None].astype(np.float32)
    pool_s = np.concatenate([r[i]["o_pool_s"] for i in range(8)], 0)[None].astype(np.float32)
    hgrn_s = np.concatenate([r[i]["o_hgrn_s"] for i in range(8)], 0)[None].astype(np.float32)
    return (y_p, y_s, conv_p, pool_p, hgrn_p, conv_s, pool_s, hgrn_s)
```

```python
import numpy as np
from contextlib import ExitStack, contextmanager
import concourse.bass as bass
import concourse.mybir as mybir
from concourse.bass_utils import run_bass_kernel_spmd

F32 = mybir.dt.float32
BF16 = mybir.dt.bfloat16
ALU = mybir.AluOpType
AF = mybir.ActivationFunctionType

D = 2048
NCH = 16
DFF = 5632
NFF = 44
TP = 512
TS = 16
T = TP + TS
HALF = T // 2
NTILE = 4
EPS = 1e-6
R_NG, R_CW, R_CB, R_LG, R_LB, R_PS, R_GN, R_HL, R_END = 0, 192, 440, 448, 456, 464, 472, 473, 505
SLOT_ELEMS = 16 * 512


class Buf:
    def __init__(self, rel=None):
        self.w = None
        self.r = dict(rel) if rel else {}


class Q:
    def __init__(self, name):
        self.name = name
        self.ops = []
        self.count = 0
        self.seen = {}


class Prog:
    def __init__(self, nc):
        self.nc = nc
        self.q = {n: Q(n) for n in ("pe", "act", "dve", "pool", "sp")}
        self.dma_cnt = {}
        self.release = {}
        self.n_inst = 0

    def buf(self):
        return Buf(self.release)

    def bufs(self, n):
        return [Buf(self.release) for _ in range(n)]

    def deps_of(self, bufs):
        out = {}
        for b in bufs:
            if b.w and out.get(b.w[0], 0) < b.w[1]:
                out[b.w[0]] = b.w[1]
            for k, v in b.r.items():
                if out.get(k, 0) < v:
                    out[k] = v
        return list(out.items())

    def _waits(self, q, reads, writes, extra):
        deps = {}

        def add(k, v):
            if deps.get(k, 0) < v:
                deps[k] = v
        for b in reads:
            if b.w:
                add(*b.w)
        for b in writes:
            if b.w:
                add(*b.w)
            for k, v in b.r.items():
                add(k, v)
        for t in extra:
            if t:
                add(*t)
        waits = []
        for k, v in deps.items():
            if q.seen.get(k, 0) < v:
                q.seen[k] = v
                waits.append((k, v))
        return waits

    def op(self, qn, fns, reads=(), writes=(), extra=(), guard=()):
        q = self.q[qn]
        if callable(fns):
            fns = [fns]
        waits = self._waits(q, reads, list(writes) + list(guard), extra)
        q.count += 1
        tok = (qn, q.count)
        q.ops.append((waits, fns, None))
        self.n_inst += len(fns)
        for b in list(reads) + list(guard):
            b.r[qn] = q.count
        for b in writes:
            b.w = tok
            b.r = {}
        return tok

    def dma(self, qn, fn, key, reads=(), writes=(), extra=()):
        q = self.q[qn]
        waits = self._waits(q, reads, writes, extra)
        k = "d:" + key
        self.dma_cnt[k] = self.dma_cnt.get(k, 0) + 16
        tok = (k, self.dma_cnt[k])
        q.ops.append((waits, [fn], k))
        self.n_inst += 1
        for b in reads:
            b.r[k] = tok[1]
        for b in writes:
            b.w = tok
            b.r = {}
        return tok

    def clock(self):
        c = {n: q.count for n, q in self.q.items() if n in ("pe", "act", "dve")}
        for k, v in self.dma_cnt.items():
            if not k.startswith("d:w"):
                c[k] = v
        return c

    @contextmanager
    def scope(self):
        es = ExitStack()
        sc = Scope(self, es)
        try:
            yield sc
        finally:
            es.close()
            self.release = self.clock()


class Scope:
    def __init__(self, P, es):
        self.P = P
        self.es = es
        self.n = 0

    def sb(self, name, shape, dt):
        self.P.uid = getattr(self.P, "uid", 0) + 1
        return self.es.enter_context(self.P.nc.sbuf_tensor("%s_u%d" % (name, self.P.uid), shape, dt))


DBG = {'ffn_phase': 9, 'nsum': 1, 'sq': 1, 'copy': 1, 'kmax': 99}


def build(ntiles=NTILE, stop_after=99):
    nc = bass.Bass("TRN2", target_bir_lowering=False)
    P = Prog(nc)

    def din(name, shape):
        return nc.dram_tensor(name, shape, F32, kind="ExternalInput").ap()

    def dout(name, shape):
        return nc.dram_tensor(name, shape, F32, kind="ExternalOutput").ap()

    xp = din("xp", [2048, D])
    xs = din("xs", [4, TS, D])
    cconv = din("cconv", [4, 30, 1024])
    cpool = din("cpool", [4, 15, 1024])
    shg = din("shg", [4, 16, 128, 128])
    ab_w_in = din("ab_w_in", [1, D, 3072])
    ab_w_out = din("ab_w_out", [1, D, D])
    conv_w = din("conv_w", [1, 31, 1024])
    conv_b = din("conv_b", [1, 1024])
    conv_ln_g = din("conv_ln_g", [1, 1024])
    conv_ln_b = din("conv_ln_b", [1, 1024])
    pool_w = din("pool_w", [1, 4, 256, 256])
    pool_scale = din("pool_scale", [1, 1024])
    hgrn_w_in = din("hgrn_w_in", [1, D, 4 * D])
    hgrn_w_out = din("hgrn_w_out", [1, D, D])
    hgrn_gnorm = din("hgrn_gnorm", [1, 128])
    hgrn_lb = din("hgrn_lb", [2, D])
    ffn_w_in = din("ffn_w_in", [2, 2, D, 2 * DFF])
    ffn_w_out = din("ffn_w_out", [2, 2, DFF, D])
    norm_g = din("norm_g", [2, 6, D])

    yp = dout("yp", [2048, D])
    ys = dout("ys", [4, TS, D])
    o_conv_p = dout("o_conv_p", [30, 1024])
    o_pool_p = dout("o_pool_p", [15, 1024])
    o_hgrn_p = dout("o_hgrn_p", [16, 128, 128])
    o_conv_s = dout("o_conv_s", [4, 30, 1024])
    o_pool_s = dout("o_pool_s", [4, 15, 1024])
    o_hgrn_s = dout("o_hgrn_s", [4, 16, 128, 128])

    dgscr = nc.dram_tensor("dgscr", [8, 128, 31 * 128], BF16).ap()
    dgscr_b = P.bufs(8)
    top = ExitStack()

    def sbp(name, shape, dt):
        return top.enter_context(nc.sbuf_tensor(name, shape, dt))

    xT = sbp("xT", [128, NCH, T], F32)
    xT_b = P.bufs(NCH)
    Sp = sbp("Sp", [128, 16, 128], F32)
    Sp_bf = sbp("Sp_bf", [128, 16, 128], BF16)
    Sp_b = P.bufs(16)
    Spbf_b = P.bufs(16)
    NSLOT = 3
    wslot = [sbp("wslot%d" % i, [128, SLOT_ELEMS], BF16) for i in range(NSLOT)]
    wslot_b = [P.bufs(4) for _ in range(NSLOT)]
    paramT = sbp("paramT", [128, 512], F32)
    par_b = P.buf()
    ones_bf = sbp("ones_bf", [128, 128], BF16)
    ones_f = sbp("ones_f", [128, 128], F32)
    ident_f = sbp("ident_f", [128, 128], F32)
    ident_bf = sbp("ident_bf", [128, 128], BF16)
    tmask = sbp("tmask", [64, 64], F32)
    rmask = sbp("rmask", [128, T], F32)
    icnt = sbp("icnt", [128, 16], F32)
    lbv = sbp("lbv", [128, 3, 16], F32)
    chist = sbp("chist", [128, 8, 30], F32)
    phist = sbp("phist", [128, 8, 15], F32)
    chist_b = P.buf()
    phist_b = P.buf()
    const_b = P.buf()
    ps = top.enter_context(nc.psum_tensor("ps", [128, 8, 512], F32))
    bank_b = P.bufs(8)
    reg_b = [P.bufs(4) for _ in range(8)]
    state = {"pair": 0, "bank": 0, "slot": 0}

    def next_pair():
        p = state["pair"]
        state["pair"] = (p + 1) % 4
        return p

    def next_bank():
        b = state["bank"]
        state["bank"] = (b + 1) % 8
        return b

    def bank_all(b):
        return [bank_b[b]]

    def pair_bufs(p):
        return bank_all(2 * p) + bank_all(2 * p + 1)

    def pview(p):
        return ps[:, 2 * p:2 * p + 2, 0:HALF]

    def sview(ap2d):
        return ap2d.rearrange("p (h c) -> p h c", h=2)

    def chain(qn, fns, reads, writes):
        tok = None
        for f in fns:
            tok = P.op(qn, f, reads=reads, writes=writes)
        return tok

    def init_consts():
        ms = [lambda e: e.memset(ones_f[:], 1.0), lambda e: e.memset(ones_bf[:], 1.0), lambda e: e.memset(rmask[:], 1.0),
              lambda e: e.memset(Sp[:], 0.0), lambda e: e.memset(Sp_bf[:], 0.0), lambda e: e.memset(chist[:], 0.0),
              lambda e: e.memset(phist[:], 0.0)]
        ms += [(lambda e, t=t: e.memset(icnt[:, t:t + 1], 1.0 / (t + 1))) for t in range(16)]
        P.op("dve", ms, writes=[const_b, chist_b, phist_b] + Sp_b + Spbf_b)
        P.op("dve", [lambda e: e.memset(rmask[:, 0:TP].rearrange("p (c k) -> p c k", k=64)[:, :, 0:1], 0.0),
                     lambda e: e.memset(rmask[:, TP:TP + 1], 0.0)], writes=[const_b])
        idb = P.buf()
        P.op("pool", lambda e: e.affine_select(out=ident_f[:], in_=ones_f[:], pattern=[[-1, 128]], compare_op=ALU.is_equal,
                                               fill=0.0, base=0, channel_multiplier=1), reads=[const_b], writes=[idb])
        P.op("pool", lambda e: e.affine_select(out=tmask[:], in_=ones_f[0:64, 0:64], pattern=[[1, 64]],
                                               compare_op=ALU.is_ge, fill=0.0, base=0, channel_multiplier=-1),
             reads=[const_b], writes=[par_b])
        P.op("act", lambda e: e.activation(out=ident_bf[:], in_=ident_f[:], func=AF.Copy), reads=[idb, par_b], writes=[const_b])
        with P.scope() as sc:
            stg = sc.sb("pstg", [128, 4, 128], F32)
            stg_b = P.buf()
            P.op("dve", lambda e: e.memset(stg[:], 0.0), writes=[stg_b])
            srcs = [(R_NG, norm_g.rearrange("l i (c p) -> (l i c) p", p=128), 192),
                    (R_CW, conv_w[0].rearrange("j (c p) -> (j c) p", p=128), 248),
                    (R_CB, conv_b[0].rearrange("(c p) -> c p", p=128), 8),
                    (R_LG, conv_ln_g[0].rearrange("(c p) -> c p", p=128), 8),
                    (R_LB, conv_ln_b[0].rearrange("(c p) -> c p", p=128), 8),
                    (R_PS, pool_scale[0].rearrange("(c p) -> c p", p=128), 8),
                    (R_GN, hgrn_gnorm, 1),
                    (R_HL, hgrn_lb.rearrange("l (c p) -> (l c) p", p=128), 32)]
            for r0, src, n in srcs:
                done = 0
                while done < n:
                    r = r0 + done
                    blk, off = r // 128, r % 128
                    cnt = min(n - done, 128 - off)
                    P.dma("sp", (lambda e, blk=blk, off=off, cnt=cnt, src=src, done=done:
                                 e.dma_start(out=stg[off:off + cnt, blk, :], in_=src[done:done + cnt, :])),
                          "par", writes=[stg_b])
                    done += cnt
            bk = next_bank()
            P.op("pe", [(lambda e, b=b: e.transpose(ps[:, bk, b * 128:(b + 1) * 128], stg[:, b, :], ident_f[:]))
                        for b in range(4)], reads=[stg_b, const_b, idb], writes=bank_all(bk))
            P.op("act", lambda e: e.activation(out=paramT[:], in_=ps[:, bk, :], func=AF.Copy),
                 reads=bank_all(bk), writes=[par_b])
            P.op("dve", lambda e: e.tensor_tensor(out=lbv[:, 1, :], in0=paramT[:, R_HL + 16:R_HL + 32],
                                                  in1=paramT[:, R_HL:R_HL + 16], op=ALU.subtract),
                 reads=[par_b], writes=[const_b])
            P.op("act", lambda e: e.activation(out=lbv[:, 0, :], in_=lbv[:, 1, :], func=AF.Sigmoid),
                 reads=[const_b], writes=[const_b])
            P.op("dve", lambda e: e.tensor_scalar(out=lbv[:, 1, :], in0=lbv[:, 0, :], scalar1=-1.0, scalar2=1.0,
                                                  op0=ALU.mult, op1=ALU.add), writes=[const_b])
            P.op("dve", lambda e: e.tensor_scalar(out=lbv[:, 2, :], in0=lbv[:, 0, :], scalar1=-1.0, scalar2=None, op0=ALU.add),
                 writes=[const_b, par_b])

    def pcol(r):
        return paramT[:, r:r + 1]

    def init_diag():
        with P.scope() as sc:
            stg = [sc.sb("dgstg%d" % i, [128, 31 * 128], BF16) for i in range(2)]
            stg_b = [P.bufs(2) for _ in range(2)]
            for c in range(8):
                i = c % 2
                P.op("dve", [(lambda e, j=j, c=c, i=i: e.tensor_scalar(out=stg[i][:, j * 128:(j + 1) * 128], in0=ident_bf[:],
                                                                       scalar1=pcol(R_CW + j * 8 + c), scalar2=None, op0=ALU.mult))
                             for j in range(0, 16)], reads=[const_b, par_b], writes=[stg_b[i][0]])
                P.op("act", [(lambda e, j=j, c=c, i=i: e.activation(out=stg[i][:, j * 128:(j + 1) * 128], in_=ident_bf[:], func=AF.Copy,
                                                                    scale=pcol(R_CW + j * 8 + c)))
                             for j in range(16, 31)], reads=[const_b, par_b], writes=[stg_b[i][1]])
                P.dma("sp", (lambda e, c=c, i=i: e.dma_start(out=dgscr[c], in_=stg[i][:])), "dgo%d" % i,
                      reads=stg_b[i], writes=[dgscr_b[c]])

    def load_raw(src2d, nelem, src_b):
        s_ = state["slot"]
        state["slot"] = (s_ + 1) % NSLOT
        deps = P.deps_of(wslot_b[s_])
        P.dma("pool", (lambda e, s_=s_: e.dma_start(out=wslot[s_][:, 0:nelem], in_=src2d)), "w%d" % s_,
              reads=[src_b], writes=[wslot_b[s_][0]], extra=deps)
        for b in wslot_b[s_][1:]:
            b.w = None
            b.r = {}
        return wslot_b[s_], wslot[s_]

    def load_w(pieces, kc_n, ncols):
        s = state["slot"]
        state["slot"] = (s + 1) % NSLOT
        view = wslot[s][:, 0:kc_n * ncols].rearrange("p (k m) -> p k m", k=kc_n)
        deps = P.deps_of(wslot_b[s])
        for i, pc in enumerate(pieces):
            src, c0, n = pc[0], pc[1], pc[2]
            k0 = pc[3] if len(pc) > 3 else 0
            kn = src.shape[0] // 128
            P.dma("pool", (lambda e, src=src, c0=c0, n=n, view=view, k0=k0, kn=kn:
                           e.dma_start(out=view[:, k0:k0 + kn, c0:c0 + n], in_=src.rearrange("(k p) m -> p k m", p=128))),
                  "w%d" % s, writes=[wslot_b[s][i]], extra=deps)
        for b in wslot_b[s][len(pieces):]:
            b.w = None
            b.r = {}
        return wslot_b[s], view

    def mm_pair(pr, lhs_list, rhs_fn, reads):
        n = len(lhs_list)
        fns = []
        for k in range(n):
            for h in range(2):
                fns.append(lambda e, k=k, h=h: e.matmul(ps[:, 2 * pr + h, 0:HALF], lhs_list[k], rhs_fn(k, h),
                                                         start=(k == 0), stop=(k == n - 1)))
        return P.op("pe", fns, reads=reads, writes=pair_bufs(pr))

    def rhs_of(t3):
        return lambda k, h: t3[:, k, h * HALF:(h + 1) * HALF]

    def finish_rstd(pr, inv_n, rstd, rstd_b, ln_bias=0.0):
        P.op("act", lambda e: e.activation(out=sview(rstd[:]), in_=pview(pr), func=AF.Ln, scale=inv_n, bias=EPS),
             reads=pair_bufs(pr), writes=[rstd_b])
        if ln_bias != 0.0:
            P.op("act", lambda e: e.activation(out=rstd[:], in_=rstd[:], func=AF.Exp, scale=-0.5, bias=ln_bias), writes=[rstd_b])
        else:
            P.op("act", lambda e: e.activation(out=rstd[:], in_=rstd[:], func=AF.Exp, scale=-0.5), writes=[rstd_b])

    def prenorm(l, i, hT, hT_b, tmp):
        sqs, sq_b, rstd, rstd_b = tmp
        pr = next_pair()
        for c in range(NCH):
            ii = c % len(sqs)
            P.op("act", (lambda e, c=c, ii=ii: e.activation(out=sqs[ii][:], in_=xT[:, c, :], func=AF.Square)),
                 reads=[xT_b[c]], writes=[sq_b[ii]])
            P.op("pe", [(lambda e, ii=ii, h=h, c=c: e.matmul(ps[:, 2 * pr + h, 0:HALF], ones_bf[:],
                                                             sqs[ii][:, h * HALF:(h + 1) * HALF],
                                                             start=(c == 0), stop=(c == NCH - 1))) for h in range(2)],
                 reads=[sq_b[ii], const_b], writes=pair_bufs(pr))
        finish_rstd(pr, 1.0 / D, rstd, rstd_b)
        for c in range(NCH):
            P.op("dve", (lambda e, c=c: e.scalar_tensor_tensor(out=hT[:, c, :], in0=xT[:, c, :],
                                                               scalar=pcol(R_NG + (l * 6 + i) * 16 + c), in1=rstd[:],
                                                               op0=ALU.mult, op1=ALU.mult)),
                 reads=[xT_b[c], rstd_b, par_b], writes=[hT_b[c]])

    def postnorm_add(l, i, hout, hout_b, nsum_pr, alpha, tmp):
        sqs, sq_b, rstd, rstd_b = tmp
        finish_rstd(nsum_pr, 1.0 / D, rstd, rstd_b, ln_bias=float(np.log(alpha)) if alpha != 1.0 else 0.0)
        for c in range(NCH):
            P.op("dve", (lambda e, c=c: e.scalar_tensor_tensor(out=hout[:, c, :], in0=hout[:, c, :],
                                                               scalar=pcol(R_NG + (l * 6 + i) * 16 + c),
                                                               in1=rstd[:], op0=ALU.mult, op1=ALU.mult)),
                 reads=[rstd_b, par_b], writes=[hout_b[c]])
        for c in range(NCH):
            P.op("dve", (lambda e, c=c: e.tensor_tensor(out=xT[:, c, :], in0=xT[:, c, :], in1=hout[:, c, :], op=ALU.add)),
                 reads=[hout_b[c]], writes=[xT_b[c]])

    def out_proj(w2d, kc_n, rhs3, rhs_b, hout, hout_b, tmp, gcols):
        sqs, sq_b, rstd, rstd_b = tmp
        nsum = next_pair()
        mi = 0
        pending = None
        for g0 in range(0, D, gcols):
            if kc_n > 16:
                kq = kc_n // 4
                pcs = [(w2d[q * kq * 128:(q + 1) * kq * 128, g0:g0 + gcols], 0, gcols, q * kq) for q in range(4)]
            else:
                pcs = [(w2d[:, g0:g0 + gcols], 0, gcols)]
            wb, wv = load_w(pcs, kc_n, gcols)
            for mc in range(gcols // 128):
                pr = next_pair()
                if pr == nsum:
                    pr = next_pair()
                mm_pair(pr, [wv[:, k, mc * 128:(mc + 1) * 128] for k in range(min(kc_n, DBG['kmax']))], rhs_of(rhs3), wb + rhs_b)
                m = mi
                ii = m % len(sqs)
                if DBG['copy']:
                    P.op("dve", (lambda e, m=m, pr=pr: e.tensor_copy(out=sview(hout[:, m, :]), in_=pview(pr))),
                         reads=pair_bufs(pr), writes=[hout_b[m]])
                if DBG['sq']:
                    P.op("act", (lambda e, ii=ii, m=m: e.activation(out=sqs[ii][:], in_=hout[:, m, :], func=AF.Square)),
                         reads=[hout_b[m]], writes=[sq_b[ii]])
                if pending is not None:
                    pending()

                def pending(ii=ii, m=m):
                    P.op("pe", [(lambda e, ii=ii, h=h, m=m: e.matmul(ps[:, 2 * nsum + h, 0:HALF], ones_bf[:],
                                                                     sqs[ii][:, h * HALF:(h + 1) * HALF],
                                                                     start=(m == 0), stop=(m == NCH - 1))) for h in range(2)],
                         reads=[sq_b[ii], const_b], writes=pair_bufs(nsum))
                mi += 1
        pending()
        return nsum

    def mk_tmp_p(tag):
        sqs = [sbp("sq%s%d" % (tag, i), [128, T], BF16) for i in range(2)]
        sq_b = P.bufs(2)
        rstd = sbp("rstd" + tag, [128, T], F32)
        return (sqs, sq_b, rstd, P.buf())
    TMP_PRE = mk_tmp_p("pre")
    TMP_POST = mk_tmp_p("post")

    def ffn(l, j, tag):
        w_in = ffn_w_in[l, j]
        w_out = ffn_w_out[l, j]
        with P.scope() as sc:
            hT = sc.sb("hT" + tag, [128, NCH, T], BF16)
            hT_b = P.bufs(NCH)
            gT = sc.sb("gT" + tag, [128, NFF, T], BF16)
            gT_b = P.bufs(NFF)
            hout = sc.sb("hout" + tag, [128, NCH, T], F32)
            hout_b = P.bufs(NCH)
            sa = [sc.sb("sa%s%d" % (tag, i), [128, T], F32) for i in range(2)]
            sa_b = P.bufs(2)
            prenorm(l, 0 if j == 0 else 4, hT, hT_b, TMP_PRE)
            for grp in range(NFF // 2 if DBG['ffn_phase'] >= 1 else 0):
                c0 = grp * 256
                wb, wv = load_w([(w_in[:, c0:c0 + 256], 0, 256), (w_in[:, DFF + c0:DFF + c0 + 256], 256, 256)], NCH, 512)
                for jj in range(2):
                    fch = grp * 2 + jj
                    pa = next_pair()
                    pb = next_pair()
                    mm_pair(pa, [wv[:, k, jj * 128:(jj + 1) * 128] for k in range(NCH)], rhs_of(hT), wb + hT_b)
                    mm_pair(pb, [wv[:, k, 256 + jj * 128:256 + (jj + 1) * 128] for k in range(NCH)], rhs_of(hT), wb + hT_b)
                    ii = fch % 2
                    P.op("act", (lambda e, ii=ii, pa=pa: e.activation(out=sview(sa[ii][:]), in_=pview(pa), func=AF.Silu)),
                         reads=pair_bufs(pa), writes=[sa_b[ii]])
                    P.op("dve", (lambda e, ii=ii, pb=pb, fch=fch: e.tensor_tensor(out=sview(gT[:, fch, :]), in0=sview(sa[ii][:]),
                                                                                  in1=pview(pb), op=ALU.mult)),
                         reads=[sa_b[ii]] + pair_bufs(pb), writes=[gT_b[fch]])
            if DBG['ffn_phase'] >= 2:
                nsum = out_proj(w_out, NFF, gT, gT_b, hout, hout_b, TMP_POST, 128)
            if DBG['ffn_phase'] >= 3:
                postnorm_add(l, 1 if j == 0 else 5, hout, hout_b, nsum, 0.5, TMP_POST)

    def load_tile(t):
        with P.scope() as sc:
            stg = [sc.sb("xstg%d" % i, [128, D], F32) for i in range(2)]
            stg_b = P.bufs(2)
            for blk in range(5):
                i = blk % 2
                ntok = 128 if blk < 4 else TS
                src = xp[t * TP + blk * 128:t * TP + (blk + 1) * 128, :] if blk < 4 else xs[t]
                P.dma("sp", (lambda e, i=i, ntok=ntok, src=src: e.dma_start(out=stg[i][0:ntok, :], in_=src)),
                      "xin%d" % i, writes=[stg_b[i]])
                for cg in range(4):
                    bk = next_bank()
                    P.op("pe", [(lambda e, c=c, i=i, ntok=ntok, bk=bk, cg=cg:
                                 e.transpose(ps[:, bk, (c - cg * 4) * 128:(c - cg * 4) * 128 + ntok],
                                             stg[i][0:ntok, c * 128:(c + 1) * 128], ident_f[0:ntok, 0:ntok]))
                                for c in range(cg * 4, cg * 4 + 4)],
                         reads=[stg_b[i], const_b], writes=bank_all(bk))
                    P.op("act", (lambda e, bk=bk, cg=cg, blk=blk, ntok=ntok:
                                 e.activation(out=xT[:, cg * 4:cg * 4 + 4, blk * 128:blk * 128 + ntok],
                                              in_=ps[:, bk, :].rearrange("p (c k) -> p c k", k=128)[:, :, 0:ntok],
                                              func=AF.Copy)),
                         reads=bank_all(bk), guard=xT_b[cg * 4:cg * 4 + 4])
            for c in range(NCH):
                xT_b[c].w = ("act", P.q["act"].count)
                xT_b[c].r = {}

    def store_tile(t):
        with P.scope() as sc:
            stg = [sc.sb("ystg%d" % i, [128, D], F32) for i in range(2)]
            stg_b = [P.bufs(4) for _ in range(2)]
            for blk in range(5):
                i = blk % 2
                ntok = 128 if blk < 4 else TS
                dst = yp[t * TP + blk * 128:t * TP + (blk + 1) * 128, :] if blk < 4 else ys[t]
                for cg in range(4):
                    bk = next_bank()
                    P.op("pe", [(lambda e, c=c, ntok=ntok, bk=bk, cg=cg, blk=blk:
                                 e.transpose(ps[0:ntok, bk, (c - cg * 4) * 128:(c - cg * 4 + 1) * 128],
                                             xT[:, c, blk * 128:blk * 128 + ntok], ident_f[:]))
                                for c in range(cg * 4, cg * 4 + 4)],
                         reads=xT_b[cg * 4:cg * 4 + 4] + [const_b], writes=bank_all(bk))
                    P.op("act", (lambda e, bk=bk, cg=cg, i=i, ntok=ntok:
                                 e.activation(out=stg[i][0:ntok, cg * 512:(cg + 1) * 512], in_=ps[0:ntok, bk, :], func=AF.Copy)),
                         reads=bank_all(bk), writes=[stg_b[i][cg]])
                P.dma("sp", (lambda e, i=i, ntok=ntok, dst=dst: e.dma_start(out=dst, in_=stg[i][0:ntok, :])),
                      "yout%d" % i, reads=stg_b[i])

    def mixer_ab(t):
        l = 0
        last = (t == ntiles - 1)
        AW = 30 + TP + 30 + TS
        PW = 15 + TP + 15 + TS
        CW = AW - 30
        CH = CW // 2
        with P.scope() as sc0:
            m_in = sc0.sb("m_in", [128, NCH, T], BF16)
            m_in_b = P.bufs(NCH)
            with P.scope() as scA:
                abuf = scA.sb("abuf", [128, 8, AW], F32)
                abuf_b = P.bufs(8)
                pbuf = scA.sb("pbuf", [128, 8, PW], F32)
                pbuf_b = P.bufs(8)
                abf = scA.sb("abf", [128, 8, AW], BF16)
                abf_b = P.bufs(8)
                with P.scope() as sc:
                    stg4 = sc.sb("stg4", [32, 2, 1024], F32)
                    stg4_b = [P.bufs(2) for _ in range(2)]
                    hT = sc.sb("hTab", [128, NCH, T], BF16)
                    hT_b = P.bufs(NCH)
                    sg = [sc.sb("sgab%d" % i, [128, T], F32) for i in range(2)]
                    sg_b = P.bufs(2)
                    prenorm(l, 2, hT, hT_b, TMP_PRE)
                    P.op("dve", lambda e: e.tensor_copy(out=abuf[:, :, 0:30], in_=chist[:]), reads=[chist_b], guard=abuf_b)
                    P.op("dve", lambda e: e.tensor_copy(out=pbuf[:, :, 0:15], in_=phist[:]), reads=[phist_b], guard=pbuf_b)
                    P.dma("sp", lambda e: e.dma_start(out=stg4[0:30, 0, :], in_=cconv[t]), "hin0", writes=stg4_b[0])
                    P.dma("sp", lambda e: e.dma_start(out=stg4[0:15, 1, :], in_=cpool[t]), "hin1", writes=stg4_b[1])
                    for which, n, dstbuf, dst_b, off in ((0, 30, abuf, abuf_b, 30 + TP), (1, 15, pbuf, pbuf_b, 15 + TP)):
                        for cg in range(2):
                            bk = next_bank()
                            P.op("pe", [(lambda e, c=c, bk=bk, cg=cg, which=which, n=n:
                                         e.transpose(ps[:, bk, (c - cg * 4) * 32:(c - cg * 4) * 32 + n],
                                                     stg4[0:n, which, c * 128:(c + 1) * 128], ident_f[0:n, 0:n]))
                                        for c in range(cg * 4, cg * 4 + 4)],
                                 reads=stg4_b[which] + [const_b], writes=bank_all(bk))
                            P.op("act", (lambda e, bk=bk, cg=cg, n=n, dstbuf=dstbuf, off=off:
                                         e.activation(out=dstbuf[:, cg * 4:cg * 4 + 4, off:off + n],
                                                      in_=ps[:, bk, 0:128].rearrange("p (c k) -> p c k", k=32)[:, :, 0:n],
                                                      func=AF.Copy)),
                                 reads=bank_all(bk), guard=dst_b[cg * 4:cg * 4 + 4])
                    w_in = ab_w_in[0]
                    for grp in range(4):
                        c0 = grp * 256
                        wb, wv = load_w([(w_in[:, c0:c0 + 256], 0, 256), (w_in[:, 1024 + c0:1024 + c0 + 256], 256, 256)], NCH, 512)
                        for jj in range(2):
                            ch = grp * 2 + jj
                            pa = next_pair()
                            pb = next_pair()
                            mm_pair(pa, [wv[:, k, jj * 128:(jj + 1) * 128] for k in range(NCH)], rhs_of(hT), wb + hT_b)
                            mm_pair(pb, [wv[:, k, 256 + jj * 128:256 + (jj + 1) * 128] for k in range(NCH)], rhs_of(hT), wb + hT_b)
                            ii = ch % 2
                            P.op("act", (lambda e, ii=ii, pb=pb: e.activation(out=sview(sg[ii][:]), in_=pview(pb), func=AF.Sigmoid)),
                                 reads=pair_bufs(pb), writes=[sg_b[ii]])
                            fa = [
                                (lambda e, ii=ii, pa=pa, ch=ch: e.tensor_tensor(out=abuf[:, ch, 30:30 + HALF], in0=sg[ii][:, 0:HALF],
                                                                                in1=ps[:, 2 * pa, 0:HALF], op=ALU.mult)),
                                (lambda e, ii=ii, pa=pa, ch=ch: e.tensor_tensor(out=abuf[:, ch, 30 + HALF:30 + TP], in0=sg[ii][:, HALF:TP],
                                                                                in1=ps[:, 2 * pa + 1, 0:TP - HALF], op=ALU.mult)),
                                (lambda e, ii=ii, pa=pa, ch=ch: e.tensor_tensor(out=abuf[:, ch, 60 + TP:60 + T], in0=sg[ii][:, TP:T],
                                                                                in1=ps[:, 2 * pa + 1, TP - HALF:HALF], op=ALU.mult))]
                            P.op("dve", fa, reads=[sg_b[ii]] + pair_bufs(pa), guard=[abuf_b[ch]])
                    for grp in range(2):
                        c0 = 2048 + grp * 512
                        wb, wv = load_w([(w_in[:, c0:c0 + 512], 0, 512)], NCH, 512)
                        for jj in range(4):
                            ch = grp * 4 + jj
                            pa = next_pair()
                            mm_pair(pa, [wv[:, k, jj * 128:(jj + 1) * 128] for k in range(NCH)], rhs_of(hT), wb + hT_b)
                            fp = [
                                (lambda e, pa=pa, ch=ch: e.activation(out=pbuf[:, ch, 15:15 + HALF], in_=ps[:, 2 * pa, 0:HALF], func=AF.Copy)),
                                (lambda e, pa=pa, ch=ch: e.activation(out=pbuf[:, ch, 15 + HALF:15 + TP], in_=ps[:, 2 * pa + 1, 0:TP - HALF],
                                                                      func=AF.Copy)),
                                (lambda e, pa=pa, ch=ch: e.activation(out=pbuf[:, ch, 30 + TP:30 + T], in_=ps[:, 2 * pa + 1, TP - HALF:HALF],
                                                                      func=AF.Copy))]
                            P.op("act", fp, reads=pair_bufs(pa), guard=[pbuf_b[ch]])
                    P.op("dve", lambda e: e.tensor_copy(out=chist[:], in_=abuf[:, :, TP:TP + 30]), writes=[chist_b] + abuf_b)
                    P.op("dve", lambda e: e.tensor_copy(out=phist[:], in_=pbuf[:, :, TP:TP + 15]), writes=[phist_b] + pbuf_b)
                    for c in range(8):
                        P.op("act", (lambda e, c=c: e.activation(out=abf[:, c, :], in_=abuf[:, c, :], func=AF.Copy)),
                             reads=[abuf_b[c]], writes=[abf_b[c]])
                with P.scope() as sco:
                    stg4 = sco.sb("stg4o", [32, 2, 1024], F32)
                    stg4_b = [P.bufs(2) for _ in range(2)]
                    outs = [(abuf, abuf_b, 30 + TP + 16, 30, o_conv_s[t], 0), (pbuf, pbuf_b, 15 + TP + 16, 15, o_pool_s[t], 1)]
                    if last:
                        outs += [(abuf, abuf_b, TP, 30, o_conv_p, 0), (pbuf, pbuf_b, TP, 15, o_pool_p, 1)]
                    for srcbuf, src_b, off, n, dst, oi in outs:
                        for cg in range(2):
                            bk = next_bank()
                            P.op("pe", [(lambda e, c=c, bk=bk, cg=cg, srcbuf=srcbuf, off=off, n=n:
                                         e.transpose(ps[0:n, bk, (c - cg * 4) * 128:(c - cg * 4 + 1) * 128],
                                                     srcbuf[:, c, off:off + n], ident_f[:]))
                                        for c in range(cg * 4, cg * 4 + 4)],
                                 reads=src_b[cg * 4:cg * 4 + 4] + [const_b], writes=bank_all(bk))
                            P.op("act", (lambda e, bk=bk, cg=cg, n=n, oi=oi:
                                         e.activation(out=stg4[0:n, oi, cg * 512:(cg + 1) * 512], in_=ps[0:n, bk, :], func=AF.Copy)),
                                 reads=bank_all(bk), writes=[stg4_b[oi][cg]])
                        P.dma("sp", (lambda e, n=n, oi=oi, dst=dst: e.dma_start(out=dst, in_=stg4[0:n, oi, :])),
                              "cout%d" % oi, reads=stg4_b[oi])
                with P.scope() as sc:
                    ybuf = sc.sb("ybuf", [128, 8, CW], F32)
                    ybuf_b = P.bufs(8)
                    for c in range(8):
                        wb, wd = load_raw(dgscr[c], 31 * 128, dgscr_b[c])
                        pr = next_pair()
                        P.op("pe", [(lambda e, j=j, h=h, pr=pr, c=c, wd=wd:
                                     e.matmul(ps[:, 2 * pr + h, 0:CH], wd[:, j * 128:(j + 1) * 128],
                                              abf[:, c, j + h * CH:j + (h + 1) * CH], start=(j == 0), stop=(j == 30)))
                                    for j in range(31) for h in range(2)],
                             reads=wb + [abf_b[c]], writes=pair_bufs(pr))
                        P.op("act", (lambda e, c=c, pr=pr: e.activation(out=ybuf[:, c, :].rearrange("p (h k) -> p h k", h=2),
                                                                        in_=ps[:, 2 * pr:2 * pr + 2, 0:CH], func=AF.Identity,
                                                                        bias=pcol(R_CB + c))),
                             reads=pair_bufs(pr) + [par_b], writes=[ybuf_b[c]])
                    with P.scope() as scp:
                        pt = [scp.sb("ptmp%d" % i, [128, 2, PW], F32) for i in range(2)]
                        pt_b = P.bufs(2)
                        pd = scp.sb("pdiff", [128, 8, T], BF16)
                        pd_b = P.bufs(8)
                        for gi in range(4):
                            win = 2 << gi
                            cs = slice(2 * gi, 2 * gi + 2)
                            bb = [pt[0][:, :, :], pt[1][:, :, :]]
                            gb = pt_b + pd_b[2 * gi:2 * gi + 2]
                            rb = pbuf_b[2 * gi:2 * gi + 2] + [const_b]
                            fl = [lambda e, bb=bb, cs=cs: e.tensor_tensor(out=bb[0][:, :, 1:PW], in0=pbuf[:, cs, 1:PW],
                                                                          in1=pbuf[:, cs, 0:PW - 1], op=ALU.add)]
                            cur, k, lo = 0, 2, 1
                            while k < win:
                                nxt = 1 - cur
                                fl.append(lambda e, bb=bb, cur=cur, nxt=nxt, lo=lo, k=k:
                                          e.tensor_tensor(out=bb[nxt][:, :, lo + k:PW], in0=bb[cur][:, :, lo + k:PW],
                                                          in1=bb[cur][:, :, lo:PW - k], op=ALU.add))
                                lo += k
                                k *= 2
                                cur = nxt
                            res = bb[cur]
                            fl.append(lambda e, res=res, cs=cs, win=win:
                                      e.scalar_tensor_tensor(out=pd[:, cs, 0:TP], in0=res[:, :, 15:15 + TP], scalar=1.0 / win,
                                                             in1=pbuf[:, cs, 15:15 + TP], op0=ALU.mult, op1=ALU.subtract))
                            fl.append(lambda e, res=res, cs=cs, win=win:
                                      e.scalar_tensor_tensor(out=pd[:, cs, TP:T], in0=res[:, :, 30 + TP:30 + T], scalar=1.0 / win,
                                                             in1=pbuf[:, cs, 30 + TP:30 + T], op0=ALU.mult, op1=ALU.subtract))
                            if t == 0:
                                for kk in range(2):
                                    cc = 2 * gi + kk
                                    fl.append(lambda e, res=res, kk=kk, win=win:
                                              e.tensor_tensor(out=res[:, kk, 15:15 + win - 1], in0=res[:, kk, 15:15 + win - 1],
                                                              in1=icnt[:, 0:win - 1], op=ALU.mult))
                                    fl.append(lambda e, res=res, kk=kk, cc=cc, win=win:
                                              e.tensor_tensor(out=pd[:, cc, 0:win - 1], in0=res[:, kk, 15:15 + win - 1],
                                                              in1=pbuf[:, cc, 15:15 + win - 1], op=ALU.subtract))
                            chain("dve", fl, rb, gb)
                        wb, wv = load_w([(pool_w[0, g_], g_ * 256, 256) for g_ in range(4)], 2, 1024)
                        for dch in range(8):
                            gi = dch // 2
                            pa = next_pair()
                            mm_pair(pa, [wv[:, k, gi * 256 + (dch % 2) * 128:gi * 256 + (dch % 2 + 1) * 128] for k in range(2)],
                                    (lambda k, h, gi=gi: pd[:, 2 * gi + k, h * HALF:(h + 1) * HALF]), wb + pd_b[2 * gi:2 * gi + 2])
                            P.op("act", (lambda e, pa=pa, dch=dch: e.activation(out=sview(m_in[:, 8 + dch, :]), in_=pview(pa), func=AF.Copy,
                                                                                scale=pcol(R_PS + dch))),
                                 reads=pair_bufs(pa) + [par_b], writes=[m_in_b[8 + dch]])
                    ysq = sc.sb("ysq", [128, 2, CW], F32)
                    ysq_b = P.bufs(2)
                    p1 = next_pair()
                    p2 = next_pair()
                    for c in range(8):
                        ii = c % 2
                        P.op("dve", (lambda e, c=c, ii=ii: e.tensor_tensor(out=ysq[:, ii, :], in0=ybuf[:, c, :], in1=ybuf[:, c, :], op=ALU.mult)),
                             reads=[ybuf_b[c]], writes=[ysq_b[ii]])
                        P.op("pe", [(lambda e, c=c, h=h: e.matmul(ps[:, 2 * p1 + h, 0:CH], ones_f[:], ybuf[:, c, h * CH:(h + 1) * CH],
                                                                  start=(c == 0), stop=(c == 7))) for h in range(2)],
                             reads=[ybuf_b[c], const_b], writes=pair_bufs(p1))
                        P.op("pe", [(lambda e, c=c, h=h, ii=ii: e.matmul(ps[:, 2 * p2 + h, 0:CH], ones_f[:], ysq[:, ii, h * CH:(h + 1) * CH],
                                                                         start=(c == 0), stop=(c == 7))) for h in range(2)],
                             reads=[ysq_b[ii], const_b], writes=pair_bufs(p2))
                    mean = sc.sb("lnmean", [128, CW], F32)
                    lrstd = sc.sb("lnrstd", [128, CW], F32)
                    mean_b = P.buf()
                    ln_b = P.buf()

                    def cview(ap2d):
                        return ap2d.rearrange("p (h c) -> p h c", h=2)
                    P.op("dve", lambda e: e.tensor_scalar(out=cview(mean[:]), in0=ps[:, 2 * p1:2 * p1 + 2, 0:CH], scalar1=1.0 / 1024,
                                                          scalar2=None, op0=ALU.mult),
                         reads=pair_bufs(p1), writes=[mean_b])
                    P.op("dve", lambda e: e.tensor_tensor(out=lrstd[:], in0=mean[:], in1=mean[:], op=ALU.mult), reads=[mean_b], writes=[ln_b])
                    P.op("dve", lambda e: e.scalar_tensor_tensor(out=cview(lrstd[:]), in0=ps[:, 2 * p2:2 * p2 + 2, 0:CH], scalar=1.0 / 1024,
                                                                 in1=cview(lrstd[:]), op0=ALU.mult, op1=ALU.subtract),
                         reads=pair_bufs(p2), writes=[ln_b])
                    P.op("dve", lambda e: e.tensor_scalar(out=lrstd[:], in0=lrstd[:], scalar1=0.0, scalar2=None, op0=ALU.max), writes=[ln_b])
                    P.op("act", lambda e: e.activation(out=lrstd[:], in_=lrstd[:], func=AF.Ln, bias=EPS), writes=[ln_b])
                    P.op("act", lambda e: e.activation(out=lrstd[:], in_=lrstd[:], func=AF.Exp, scale=-0.5), writes=[ln_b])
                    for c in range(8):
                        P.op("dve", (lambda e, c=c: e.tensor_tensor(out=ybuf[:, c, :], in0=ybuf[:, c, :], in1=mean[:], op=ALU.subtract)),
                             reads=[mean_b], writes=[ybuf_b[c]])
                    for c in range(8):
                        P.op("dve", (lambda e, c=c: e.tensor_tensor(out=ybuf[:, c, :], in0=ybuf[:, c, :], in1=lrstd[:], op=ALU.mult)),
                             reads=[ln_b], writes=[ybuf_b[c]])
                        P.op("act", [(lambda e, c=c: e.activation(out=m_in[:, c, 0:TP], in_=ybuf[:, c, 0:TP], func=AF.Silu,
                                                                  scale=pcol(R_LG + c), bias=pcol(R_LB + c))),
                                     (lambda e, c=c: e.activation(out=m_in[:, c, TP:T], in_=ybuf[:, c, 30 + TP:30 + T], func=AF.Silu,
                                                                  scale=pcol(R_LG + c), bias=pcol(R_LB + c)))],
                             reads=[ybuf_b[c], par_b], writes=[m_in_b[c]])
            with P.scope() as sc:
                hout = sc.sb("houtab", [128, NCH, T], F32)
                hout_b = P.bufs(NCH)
                nsum = out_proj(ab_w_out[0], NCH, m_in, m_in_b, hout, hout_b, TMP_POST, 512)
                postnorm_add(l, 3, hout, hout_b, nsum, 1.0, TMP_POST)

    def mixer_hgrn(t):
        l = 1
        last = (t == ntiles - 1)
        w_in = hgrn_w_in[0]
        chunks = [(ci * 64, 64) for ci in range(8)] + [(TP, TS)]
        with P.scope() as sc0:
            onT = sc0.sb("onT", [128, NCH, T], BF16)
            onT_b = P.bufs(NCH)
            with P.scope() as sc:
                hT = sc.sb("hThg", [128, NCH, T], BF16)
                hT_b = P.bufs(NCH)
                Ss = sc.sb("Ss", [128, 16, 128], F32)
                Ss_bf = sc.sb("Ss_bf", [128, 16, 128], BF16)
                Ss_b = P.bufs(16)
                Ssbf_b = P.bufs(16)
                P.dma("sp", lambda e: e.dma_start(out=Ss[:], in_=shg[t].rearrange("h k v -> k h v")), "sin", writes=Ss_b)
                P.op("act", lambda e: e.activation(out=Ss_bf[:], in_=Ss[:], func=AF.Copy), reads=Ss_b, writes=Ssbf_b)
                qts = [sc.sb("qt%d" % i, [128, 4, T], BF16) for i in range(2)]
                qt_bs = [P.bufs(4) for _ in range(2)]
                kt = sc.sb("kt", [128, 4, T], BF16)
                sgz = sc.sb("sgz", [128, 4, T], BF16)
                ebl = sc.sb("ebl", [128, 4, 16], F32)
                vtok = sc.sb("vtok", [64, 9, 512], BF16)
                oT = sc.sb("oT", [128, 4, T], F32)
                kt_b, sgz_b, ebl_b, oT_b = P.bufs(4), P.bufs(4), P.bufs(4), P.bufs(4)
                vtok_b = P.bufs(9)
                tm = [sc.sb("hgt%d" % i, [128, T], F32) for i in range(7)]
                tm_b = P.bufs(7)
                tm0x = [tm[0], sc.sb("hgt0b", [128, T], F32)]
                tm0x_b = [tm_b[0], P.buf()]
                khat = [sc.sb("khat%d" % i, [128, 64], BF16) for i in range(4)]
                khat_b = P.bufs(4)
                asb = [sc.sb("asb%d" % i, [64, 64], BF16) for i in range(4)]
                asb_b = P.bufs(4)
                ktk = [sc.sb("ktk%d" % i, [64, 128], BF16) for i in range(4)]
                ktk_b = P.bufs(4)
                prenorm(l, 2, hT, hT_b, TMP_PRE)
                def q_thunks(g, qt, qt_b):
                    wb, wq = load_w([(w_in[:, g * 512:(g + 1) * 512], 0, 512)], NCH, 512)

                    def mk(hl):
                        def th():
                            pr = next_pair()
                            mm_pair(pr, [wq[:, k, hl * 128:(hl + 1) * 128] for k in range(NCH)], rhs_of(hT), wb + hT_b)
                            P.op("act", (lambda e, pr=pr, hl=hl: e.activation(out=sview(qt[:, hl, :]), in_=pview(pr), func=AF.Silu)),
                                 reads=pair_bufs(pr), writes=[qt_b[hl]])
                        return th
                    return [mk(hl) for hl in range(4)]

                def gz_thunks(g):
                    wb, wg = load_w([(w_in[:, 3 * D + g * 512:3 * D + (g + 1) * 512], 0, 512)], NCH, 512)

                    def mk(hl):
                        def th():
                            pr = next_pair()
                            mm_pair(pr, [wg[:, k, hl * 128:(hl + 1) * 128] for k in range(NCH)], rhs_of(hT), wb + hT_b)
                            P.op("act", (lambda e, pr=pr, hl=hl: e.activation(out=sview(sgz[:, hl, :]), in_=pview(pr), func=AF.Silu)),
                                 reads=pair_bufs(pr), writes=[sgz_b[hl]])
                        return th
                    return [mk(hl) for hl in range(4)]

                def do_group(g, qt, qt_b, qt_n, qt_nb):
                    wb, wf = load_w([(w_in[:, D + g * 512:D + (g + 1) * 512], 0, 512)], NCH, 512)
                    for hl in range(4):
                        h = 4 * g + hl
                        pr = next_pair()
                        mm_pair(pr, [wf[:, k, hl * 128:(hl + 1) * 128] for k in range(NCH)], rhs_of(hT), wb + hT_b)
                        t0 = tm0x[hl % 2]
                        t0_b = tm0x_b[hl % 2]
                        P.op("act", (lambda e, pr=pr, t0=t0: e.activation(out=sview(t0[:]), in_=pview(pr), func=AF.Sigmoid, scale=-1.0)),
                             reads=pair_bufs(pr), writes=[t0_b])
                        P.op("dve", (lambda e, h=h, t0=t0: e.tensor_scalar(out=tm[1][:], in0=t0[:], scalar1=lbv[:, 2, h:h + 1], scalar2=1.0,
                                                                           op0=ALU.mult, op1=ALU.add)),
                             reads=[t0_b, const_b], writes=[tm_b[1]])
                        P.op("act", lambda e: e.activation(out=tm[5][:], in_=tm[1][:], func=AF.Ln), reads=[tm_b[1]], writes=[tm_b[5]])
                        P.op("dve", lambda e: e.tensor_tensor_scan(out=tm[2][:], data0=rmask[:], data1=tm[5][:], initial=0.0,
                                                                   op0=ALU.mult, op1=ALU.add),
                             reads=[tm_b[5], const_b], writes=[tm_b[2]])
                        P.op("act", [lambda e: e.activation(out=tm[3][:], in_=tm[2][:], func=AF.Exp),
                                     lambda e: e.activation(out=tm[4][:], in_=tm[2][:], func=AF.Exp, scale=-1.0)],
                             reads=[tm_b[2]], writes=[tm_b[3], tm_b[4]])
                        fk = [(lambda e, h=h, hl=hl, t0=t0: e.scalar_tensor_tensor(out=kt[:, hl, :], in0=t0[:], scalar=lbv[:, 1, h:h + 1],
                                                                            in1=tm[4][:], op0=ALU.mult, op1=ALU.mult)),
                              (lambda e, hl=hl: e.scalar_tensor_tensor(out=qt[:, hl, :], in0=qt[:, hl, :], scalar=float(128 ** -0.5),
                                                                       in1=tm[3][:], op0=ALU.mult, op1=ALU.mult)),
                              (lambda e, hl=hl: e.tensor_copy(out=ebl[:, hl, 0:8],
                                                              in_=tm[3][:, 0:TP].rearrange("p (c k) -> p c k", k=64)[:, :, 63])),
                              (lambda e, hl=hl: e.tensor_copy(out=ebl[:, hl, 8:9], in_=tm[3][:, T - 1:T]))]
                        P.op("dve", fk, reads=[t0_b, tm_b[3], tm_b[4], const_b], writes=[kt_b[hl], qt_b[hl], ebl_b[hl]])
                    wb, wvv = load_w([(w_in[:, 2 * D + g * 512:2 * D + (g + 1) * 512], 0, 512)], NCH, 512)
                    for ci, (c0, cn) in enumerate(chunks):
                        bk = next_bank()
                        P.op("pe", [(lambda e, k=k, bk=bk, c0=c0, cn=cn, wvv=wvv: e.matmul(ps[0:cn, bk, :], hT[:, k, c0:c0 + cn], wvv[:, k, :],
                                                                                  start=(k == 0), stop=(k == NCH - 1)))
                                    for k in range(NCH)], reads=wb + hT_b, writes=bank_all(bk))
                        P.op("act", (lambda e, bk=bk, ci=ci, cn=cn: e.activation(out=vtok[0:cn, ci, :], in_=ps[0:cn, bk, :], func=AF.Copy)),
                             reads=bank_all(bk), writes=[vtok_b[ci]])
                    extra = gz_thunks(g)
                    if g + 1 < 4:
                        extra = extra + q_thunks(g + 1, qt_n, qt_nb)
                    steps = [(ci, hl) for ci in range(9) for hl in range(4)]
                    ctx = {}

                    def stepA(idx):
                        ci, hl = steps[idx]
                        c0, cn = chunks[ci]
                        r = idx % 4
                        bk = next_bank()
                        bka = next_bank()
                        ctx[idx] = (bk, bka, r)
                        P.op("dve", (lambda e, r=r, hl=hl, c0=c0, cn=cn, ci=ci:
                                     e.tensor_scalar(out=khat[r][:, 0:cn], in0=kt[:, hl, c0:c0 + cn], scalar1=ebl[:, hl, ci:ci + 1],
                                                     scalar2=None, op0=ALU.mult)),
                             reads=[kt_b[hl], ebl_b[hl]], writes=[khat_b[r]])
                        P.op("pe", (lambda e, bk=bk, hl=hl, c0=c0, cn=cn:
                                    e.matmul(ps[0:cn, bk, 0:cn], kt[:, hl, c0:c0 + cn], qt[:, hl, c0:c0 + cn], start=True, stop=True)),
                             reads=[kt_b[hl], qt_b[hl]], writes=[bank_b[bk]])
                        P.op("pe", (lambda e, bka=bka, r=r, cn=cn:
                                    e.matmul(ps[0:cn, bka, 64:192], khat[r][:, 0:cn], ident_bf[:], start=True, stop=True)),
                             reads=[khat_b[r], const_b], writes=[bank_b[bka]])
                        P.op("dve", (lambda e, bk=bk, r=r, cn=cn:
                                     e.tensor_tensor(out=asb[r][0:cn, 0:cn], in0=ps[0:cn, bk, 0:cn], in1=tmask[0:cn, 0:cn], op=ALU.mult)),
                             reads=[bank_b[bk], par_b], writes=[asb_b[r]])
                        P.op("act", (lambda e, bka=bka, r=r, cn=cn:
                                     e.activation(out=ktk[r][0:cn, :], in_=ps[0:cn, bka, 64:192], func=AF.Copy)),
                             reads=[bank_b[bka]], writes=[ktk_b[r]])

                    def stepB(idx):
                        ci, hl = steps[idx]
                        c0, cn = chunks[ci]
                        bk, bka, r = ctx[idx]
                        h = 4 * g + hl
                        if ci < 8:
                            S, Sbf, S_b, Sbf_b = Sp, Sp_bf, Sp_b[h], Spbf_b[h]
                        else:
                            S, Sbf, S_b, Sbf_b = Ss, Ss_bf, Ss_b[h], Ssbf_b[h]
                        fo = [lambda e: e.matmul(ps[:, bka, 192:192 + cn], vtok[0:cn, ci, hl * 128:(hl + 1) * 128], asb[r][0:cn, 0:cn],
                                                 start=True, stop=False),
                              lambda e: e.matmul(ps[:, bka, 192:192 + cn], Sbf[:, h, :], qt[:, hl, c0:c0 + cn], start=False, stop=True)]
                        P.op("pe", fo, reads=[vtok_b[ci], asb_b[r], Sbf_b, qt_b[hl]], writes=[bank_b[bka]])
                        P.op("pe", lambda e: e.matmul(ps[:, bk, 256:384], ktk[r][0:cn, :], vtok[0:cn, ci, hl * 128:(hl + 1) * 128],
                                                      start=True, stop=True),
                             reads=[vtok_b[ci], ktk_b[r]], writes=[bank_b[bk]])
                        P.op("act", (lambda e: e.activation(out=oT[:, hl, c0:c0 + cn], in_=ps[:, bka, 192:192 + cn], func=AF.Copy)),
                             reads=[bank_b[bka]], guard=[oT_b[hl]])
                        P.op("dve", (lambda e: e.scalar_tensor_tensor(out=S[:, h, :], in0=S[:, h, :], scalar=ebl[:, hl, ci:ci + 1],
                                                                      in1=ps[:, bk, 256:384], op0=ALU.mult, op1=ALU.add)),
                             reads=[bank_b[bk], ebl_b[hl]], writes=[S_b])
                        P.op("act", (lambda e: e.activation(out=Sbf[:, h, :], in_=S[:, h, :], func=AF.Copy)),
                             reads=[S_b], writes=[Sbf_b])

                    n = len(steps)
                    every = max(1, (n - 2) // len(extra))
                    LOOK = 2
                    for i0 in range(LOOK):
                        stepA(i0)
                    for idx in range(n):
                        if idx + LOOK < n:
                            stepA(idx + LOOK)
                        stepB(idx)
                        if extra and idx % every == every - 1:
                            extra.pop(0)()
                    while extra:
                        extra.pop(0)()
                    for hl in range(4):
                        h = 4 * g + hl
                        pr = next_pair()
                        oT_b[hl].w = ("act", P.q["act"].count)
                        oT_b[hl].r = {}
                        P.op("act", (lambda e, hl=hl: e.activation(out=tm[0][:], in_=oT[:, hl, :], func=AF.Square)),
                             reads=[oT_b[hl]], writes=[tm_b[0]])
                        P.op("pe", [(lambda e, h_=h_, pr=pr: e.matmul(ps[:, 2 * pr + h_, 0:HALF], ones_f[:], tm[0][:, h_ * HALF:(h_ + 1) * HALF],
                                                                      start=True, stop=True)) for h_ in range(2)],
                             reads=[tm_b[0], const_b], writes=pair_bufs(pr))
                        finish_rstd(pr, 1.0 / 128, tm[1], tm_b[1])
                        P.op("dve", (lambda e, hl=hl: e.scalar_tensor_tensor(out=tm[6][:], in0=oT[:, hl, :], scalar=pcol(R_GN), in1=tm[1][:],
                                                                             op0=ALU.mult, op1=ALU.mult)),
                             reads=[oT_b[hl], tm_b[1], par_b], writes=[tm_b[6]])
                        P.op("dve", (lambda e, hl=hl, h=h: e.tensor_tensor(out=onT[:, h, :], in0=tm[6][:], in1=sgz[:, hl, :], op=ALU.mult)),
                             reads=[tm_b[6], sgz_b[hl]], writes=[onT_b[h]])
                for th in q_thunks(0, qts[0], qt_bs[0]):
                    th()
                for g in range(4):
                    do_group(g, qts[g % 2], qt_bs[g % 2], qts[(g + 1) % 2], qt_bs[(g + 1) % 2])
                P.dma("sp", lambda e: e.dma_start(out=o_hgrn_s[t].rearrange("h k v -> k h v"), in_=Ss[:]), "sout", reads=Ss_b)
                if last:
                    P.dma("sp", lambda e: e.dma_start(out=o_hgrn_p.rearrange("h k v -> k h v"), in_=Sp[:]), "sout", reads=Sp_b)
            with P.scope() as sc:
                hout = sc.sb("houthg", [128, NCH, T], F32)
                hout_b = P.bufs(NCH)
                nsum = out_proj(hgrn_w_out[0], NCH, onT, onT_b, hout, hout_b, TMP_POST, 512)
                postnorm_add(l, 3, hout, hout_b, nsum, 1.0, TMP_POST)

    init_consts()
    init_diag()
    stage_list = [("ffn", 0, 0), ("ab",), ("ffn", 0, 1), ("ffn", 1, 0), ("hg",), ("ffn", 1, 1)]
    for t in range(ntiles):
        load_tile(t)
        for si, st in enumerate(stage_list):
            if si >= stop_after:
                break
            if st[0] == "ffn":
                ffn(st[1], st[2], "f")
            elif st[0] == "ab":
                mixer_ab(t)
            else:
                mixer_hgrn(t)
        store_tile(t)

    es = ExitStack()
    sems = {}
    for n in ("pe", "act", "dve", "pool", "sp"):
        sems[n] = es.enter_context(nc.semaphore("s_" + n))
    for k in P.dma_cnt:
        sems[k] = es.enter_context(nc.semaphore("s_" + k.replace(":", "_")))
    final_waits = [(k, v) for k, v in P.dma_cnt.items()] + [(n, P.q[n].count) for n in ("pe", "act", "dve")]
    block = es.enter_context(nc.Block())

    def run_q(e, qn, final=False):
        q = P.q[qn]
        for waits, fns, kind in q.ops:
            for k, v in waits:
                e.wait_ge(sems[k], v)
            ins = None
            for f in fns:
                ins = f(e)
            if kind is None:
                ins.then_inc(sems[qn], 1)
            else:
                ins.then_inc(sems[kind], 16)
        if final:
            for k, v in final_waits:
                if v > 0:
                    e.wait_ge(sems[k], v)

    @block.tensor
    def _(e):
        run_q(e, "pe")

    @block.scalar
    def _(e):
        run_q(e, "act")

    @block.vector
    def _(e):
        run_q(e, "dve")

    @block.gpsimd
    def _(e):
        run_q(e, "pool")

    @block.sync
    def _(e):
        run_q(e, "sp", final=True)

    es.close()
    top.close()
    return nc, P


_W_NAMES = ["ab_w_in", "ab_w_out", "conv_w", "conv_b", "conv_ln_g", "conv_ln_b", "pool_w", "pool_scale",
            "hgrn_w_in", "hgrn_w_out", "hgrn_gnorm", "hgrn_lb", "ffn_w_in", "ffn_w_out", "norm_g"]


def make_in_maps(inputs, cores):
    f = lambda a: np.ascontiguousarray(np.asarray(a, dtype=np.float32))
    w = {n: f(inputs[n]) for n in _W_NAMES}
    maps = []
    for i in cores:
        m = dict(w)
        m["xp"] = f(inputs["x_prompt"][i])
        m["xs"] = f(inputs["x_sample"][4 * i:4 * i + 4])
        m["cconv"] = f(inputs["cache_conv"][0, 4 * i:4 * i + 4])
        m["cpool"] = f(inputs["cache_pool"][0, 4 * i:4 * i + 4])
        m["shg"] = f(inputs["state_hgrn"][0, 4 * i:4 * i + 4])
        maps.append(m)
    return maps


def kernel(**inputs):
    nc, _ = build()
    maps = make_in_maps(inputs, list(range(8)))
    res = run_bass_kernel_spmd(nc, maps, core_ids=list(range(8)))
    r = res.results
    y_p = np.stack([r[i]["yp"] for i in range(8)], 0).astype(np.float32)
    y_s = np.concatenate([r[i]["ys"] for i in range(8)], 0).astype(np.float32)
    conv_p = np.stack([r[i]["o_conv_p"] for i in range(8)], 0)[None].astype(np.float32)
    pool_p = np.stack([r[i]["o_pool_p"] for i in range(8)], 0)[None].astype(np.float32)
    hgrn_p = np.stack([r[i]["o_hgrn_p"] for i in range(8)], 0)[None].astype(np.float32)
    conv_s = np.concatenate([r[i]["o_conv_s"] for i in range(8)], 0)[None].astype(np.float32)
    pool_s = np.concatenate([r[i]["o_pool_s"] for i in range(8)], 0)[None].astype(np.float32)
    hgrn_s = np.concatenate([r[i]["o_hgrn_s"] for i in range(8)], 0)[None].astype(np.float32)
    return (y_p, y_s, conv_p, pool_p, hgrn_p, conv_s, pool_s, hgrn_s)
```
